# Optimizing a Trainium2 kernel written in Bass

```python
import math
import jax, jax.numpy as jnp
from jax import lax
import numpy as np

D_MODEL = 1024
BATCH = 8
SEQ = 2048
DEPTH = 2
DEC_BATCH = 128
DEC_SEQ = 4
PAST_LEN = 16384
PAGE_SIZE = 128

N_META = 16
D_MIX = 2 * D_MODEL
D_SSD = D_MIX // 2
D_LRU = D_MIX - D_SSD
SSD_HEAD_DIM = 64
SSD_HEADS = D_SSD // SSD_HEAD_DIM
SSD_GROUPS = 2
SSD_STATE = 128
SSD_CHUNK = 128
CONV_W = 4
SSD_CONV_DIM = D_SSD + 2 * SSD_GROUPS * SSD_STATE
LRU_BLOCKS = 16
LRU_BLOCK_W = D_LRU // LRU_BLOCKS
LRU_C = 8.0
D_FF = ((8 * D_MODEL // 3 + 127) // 128) * 128
IN_COLS = D_SSD + SSD_CONV_DIM + SSD_HEADS + 2 * D_LRU
EPS = 1e-6

kernel_name = "hymba_ssd_rglru_macaron_step"


def rmsnorm(x, g):
    xf = x.astype(jnp.float32)
    y = xf * lax.rsqrt(jnp.mean(xf * xf, axis=-1, keepdims=True) + EPS) * g.astype(jnp.float32)
    return y.astype(x.dtype)


def swiglu(x, wg, wu, wd):
    return (jax.nn.silu(x @ wg) * (x @ wu)) @ wd


def causal_conv(x, buf, w, b):
    L = x.shape[1]
    xp = jnp.concatenate([buf.astype(x.dtype), x], axis=1)
    y = b + sum(xp[:, k:k + L] * w[k] for k in range(CONV_W))
    return y, xp[:, -(CONV_W - 1):]


def ssd_segment(x, dt, a, bm, cm, h0, chunk):
    b, L, H, P = x.shape
    G, N = bm.shape[2], bm.shape[3]
    R = H // G
    c = L // chunk
    xc = x.astype(jnp.float32).reshape(b, c, chunk, G, R, P)
    dtc = dt.reshape(b, c, chunk, G, R)
    bc = bm.astype(jnp.float32).reshape(b, c, chunk, G, N)
    cc = cm.astype(jnp.float32).reshape(b, c, chunk, G, N)
    cum = jnp.cumsum(dtc * a.reshape(G, R), axis=2)
    diff = cum[:, :, :, None] - cum[:, :, None, :]
    mask = jnp.tril(jnp.ones((chunk, chunk), bool))[:, :, None, None]
    decay = jnp.exp(jnp.where(mask, diff, -jnp.inf))
    cb = jnp.einsum('bcign,bcjgn->bcijg', cc, bc)
    scores = cb[..., None] * decay * dtc[:, :, None]
    y_diag = jnp.einsum('bcijgr,bcjgrp->bcigrp', scores, xc)
    w_end = jnp.exp(cum[:, :, -1:] - cum) * dtc
    states = jnp.einsum('bcjgn,bcjgr,bcjgrp->bcgrpn', bc, w_end, xc)
    chunk_decay = jnp.exp(cum[:, :, -1])

    def step(h, inp):
        s, d = inp
        return d[..., None, None] * h + s, h

    h_final, h_prev = lax.scan(step, h0.astype(jnp.float32).reshape(b, G, R, P, N),
                               (jnp.moveaxis(states, 1, 0), jnp.moveaxis(chunk_decay, 1, 0)))
    h_prev = jnp.moveaxis(h_prev, 0, 1)
    y_off = jnp.einsum('bcign,bcgrpn,bcigr->bcigrp', cc, h_prev, jnp.exp(cum))
    y = (y_diag + y_off).reshape(b, L, H, P)
    return y, h_final.reshape(b, H, P, N)


def mixer(u, ssd_h, ssd_cbuf, lru_h, lru_cbuf, segments, p):
    b, L, _ = u.shape
    proj = u @ p['w_in']
    z, xbc, dt_raw, gate, xr = jnp.split(
        proj, [D_SSD, D_SSD + SSD_CONV_DIM, D_SSD + SSD_CONV_DIM + SSD_HEADS,
               D_SSD + SSD_CONV_DIM + SSD_HEADS + D_LRU], axis=-1)
    xbc_c, new_ssd_cbuf = causal_conv(xbc, ssd_cbuf, p['ssd_conv_w'], p['ssd_conv_b'])
    xbc_c = jax.nn.silu(xbc_c)
    xs, bm, cm = jnp.split(xbc_c, [D_SSD, D_SSD + SSD_GROUPS * SSD_STATE], axis=-1)
    dt = jax.nn.softplus(dt_raw.astype(jnp.float32) + p['ssd_dt_bias'].astype(jnp.float32))
    a = -jnp.exp(p['ssd_a_log'].astype(jnp.float32))
    xh = xs.reshape(b, L, SSD_HEADS, SSD_HEAD_DIM)
    bm = bm.reshape(b, L, SSD_GROUPS, SSD_STATE)
    cm = cm.reshape(b, L, SSD_GROUPS, SSD_STATE)
    h = ssd_h
    ys = []
    start = 0
    for length, chunk in segments:
        y_seg, h = ssd_segment(xh[:, start:start + length], dt[:, start:start + length], a,
                               bm[:, start:start + length], cm[:, start:start + length], h, chunk)
        ys.append(y_seg)
        start += length
    y = jnp.concatenate(ys, axis=1) if len(ys) > 1 else ys[0]
    y = y + p['ssd_d'].astype(jnp.float32)[:, None] * xh.astype(jnp.float32)
    y = y.reshape(b, L, D_SSD) * jax.nn.silu(z.astype(jnp.float32))
    y_ssd = rmsnorm(y, p['ssd_norm_g'])
    xr_c, new_lru_cbuf = causal_conv(xr, lru_cbuf, p['lru_conv_w'], p['lru_conv_b'])
    xb = xr_c.reshape(b, L, LRU_BLOCKS, LRU_BLOCK_W)
    r = jax.nn.sigmoid((jnp.einsum('blki,kij->blkj', xb, p['lru_wa']).reshape(b, L, D_LRU)
                        + p['lru_ba']).astype(jnp.float32))
    ig = jax.nn.sigmoid((jnp.einsum('blki,kij->blkj', xb, p['lru_wx']).reshape(b, L, D_LRU)
                         + p['lru_bx']).astype(jnp.float32))
    log_a = -LRU_C * r * jax.nn.softplus(-p['lru_lambda'].astype(jnp.float32))
    a_t = jnp.exp(log_a)
    mult = jnp.sqrt(-jnp.expm1(2.0 * log_a))
    bt = mult * ig * xr_c.astype(jnp.float32)
    bt = bt.at[:, 0].add(a_t[:, 0] * lru_h.astype(jnp.float32))

    def comb(left, right):
        a1, b1 = left
        a2, b2 = right
        return a1 * a2, a2 * b1 + b2

    _, h_all = lax.associative_scan(comb, (a_t, bt), axis=1)
    y_lru = h_all * jax.nn.gelu(gate.astype(jnp.float32))
    out = jnp.concatenate([y_ssd.astype(jnp.float32), y_lru], axis=-1).astype(u.dtype) @ p['w_out']
    return out, h, new_ssd_cbuf, h_all[:, -1], new_lru_cbuf


def trunk_layer(x, ssd_h, ssd_cbuf, lru_h, lru_cbuf, segments, p):
    x = x + 0.5 * rmsnorm(swiglu(rmsnorm(x, p['ffn1_pre_g']), p['ffn1_wg'], p['ffn1_wu'], p['ffn1_wd']),
                          p['ffn1_post_g'])
    m, ssd_h, ssd_cbuf, lru_h, lru_cbuf = mixer(rmsnorm(x, p['mix_pre_g']), ssd_h, ssd_cbuf, lru_h,
                                                lru_cbuf, segments, p)
    x = x + rmsnorm(m, p['mix_post_g'])
    x = x + 0.5 * rmsnorm(swiglu(rmsnorm(x, p['ffn2_pre_g']), p['ffn2_wg'], p['ffn2_wu'], p['ffn2_wd']),
                          p['ffn2_post_g'])
    return x, ssd_h, ssd_cbuf, lru_h, lru_cbuf


def setup_inputs(seed: int = 0) -> dict:
    key = jax.random.key(seed)
    ks = jax.random.split(key, 40)
    f32 = jnp.float32
    L = DEPTH

    def nrm(k, shape, scale):
        return jax.random.normal(k, shape, f32) * scale

    def gain(k, shape):
        return 1.0 + 0.05 * jax.random.normal(k, shape, f32)

    dt0 = jnp.exp(jax.random.uniform(ks[17], (L, SSD_HEADS), f32, math.log(1e-3), math.log(1e-1)))
    a0 = jax.random.uniform(ks[27], (L, D_LRU), f32, 0.9, 0.999)
    s0 = a0 ** (1.0 / LRU_C)
    return {
        "x_prompt": nrm(ks[0], (BATCH, SEQ, D_MODEL), 1.0),
        "x_sample": nrm(ks[1], (DEC_BATCH, DEC_SEQ, D_MODEL), 1.0),
        "state_ssd": nrm(ks[2], (L, DEC_BATCH, SSD_HEADS, SSD_HEAD_DIM, SSD_STATE), 0.1),
        "state_ssd_conv": nrm(ks[3], (L, DEC_BATCH, CONV_W - 1, SSD_CONV_DIM), 1.0),
        "state_lru": nrm(ks[4], (L, DEC_BATCH, D_LRU), 0.5),
        "state_lru_conv": nrm(ks[5], (L, DEC_BATCH, CONV_W - 1, D_LRU), 1.0),
        "meta_tokens": nrm(ks[6], (N_META, D_MODEL), 1.0),
        "ffn1_pre_g": gain(ks[7], (L, D_MODEL)),
        "ffn1_post_g": gain(ks[8], (L, D_MODEL)),
        "ffn1_wg": nrm(ks[9], (L, D_MODEL, D_FF), D_MODEL ** -0.5),
        "ffn1_wu": nrm(ks[10], (L, D_MODEL, D_FF), D_MODEL ** -0.5),
        "ffn1_wd": nrm(ks[11], (L, D_FF, D_MODEL), D_FF ** -0.5),
        "mix_pre_g": gain(ks[12], (L, D_MODEL)),
        "mix_post_g": gain(ks[13], (L, D_MODEL)),
        "w_in": nrm(ks[14], (L, D_MODEL, IN_COLS), D_MODEL ** -0.5),
        "ssd_conv_w": nrm(ks[15], (L, CONV_W, SSD_CONV_DIM), CONV_W ** -0.5),
        "ssd_conv_b": nrm(ks[16], (L, SSD_CONV_DIM), 0.01),
        "ssd_dt_bias": dt0 + jnp.log(-jnp.expm1(-dt0)),
        "ssd_a_log": jnp.log(jax.random.uniform(ks[18], (L, SSD_HEADS), f32, 1.0, 16.0)),
        "ssd_d": gain(ks[19], (L, SSD_HEADS)),
        "ssd_norm_g": gain(ks[20], (L, D_SSD)),
        "lru_conv_w": nrm(ks[21], (L, CONV_W, D_LRU), CONV_W ** -0.5),
        "lru_conv_b": nrm(ks[22], (L, D_LRU), 0.01),
        "lru_wa": nrm(ks[23], (L, LRU_BLOCKS, LRU_BLOCK_W, LRU_BLOCK_W), LRU_BLOCK_W ** -0.5),
        "lru_ba": nrm(ks[24], (L, D_LRU), 0.01),
        "lru_wx": nrm(ks[25], (L, LRU_BLOCKS, LRU_BLOCK_W, LRU_BLOCK_W), LRU_BLOCK_W ** -0.5),
        "lru_bx": nrm(ks[26], (L, D_LRU), 0.01),
        "lru_lambda": jnp.log(s0) - jnp.log1p(-s0),
        "w_out": nrm(ks[28], (L, D_MIX, D_MODEL), D_MIX ** -0.5),
        "ffn2_pre_g": gain(ks[29], (L, D_MODEL)),
        "ffn2_post_g": gain(ks[30], (L, D_MODEL)),
        "ffn2_wg": nrm(ks[31], (L, D_MODEL, D_FF), D_MODEL ** -0.5),
        "ffn2_wu": nrm(ks[32], (L, D_MODEL, D_FF), D_MODEL ** -0.5),
        "ffn2_wd": nrm(ks[33], (L, D_FF, D_MODEL), D_FF ** -0.5),
    }


def reference(x_prompt, x_sample, state_ssd, state_ssd_conv, state_lru, state_lru_conv, meta_tokens,
              ffn1_pre_g, ffn1_post_g, ffn1_wg, ffn1_wu, ffn1_wd, mix_pre_g, mix_post_g, w_in,
              ssd_conv_w, ssd_conv_b, ssd_dt_bias, ssd_a_log, ssd_d, ssd_norm_g, lru_conv_w, lru_conv_b,
              lru_wa, lru_ba, lru_wx, lru_bx, lru_lambda, w_out, ffn2_pre_g, ffn2_post_g, ffn2_wg,
              ffn2_wu, ffn2_wd):
    bp, seq, _ = x_prompt.shape
    bs, dec_seq, _ = x_sample.shape
    xp = jnp.concatenate([jnp.broadcast_to(meta_tokens.astype(x_prompt.dtype), (bp, N_META, D_MODEL)),
                          x_prompt], axis=1)
    prompt_segments = [(N_META, N_META), (seq, min(SSD_CHUNK, seq))]
    sample_segments = [(dec_seq, dec_seq)]
    xs = x_sample
    p_ssd, p_ssdc, p_lru, p_lruc = [], [], [], []
    s_ssd, s_ssdc, s_lru, s_lruc = [], [], [], []
    for l in range(DEPTH):
        p = dict(ffn1_pre_g=ffn1_pre_g[l], ffn1_post_g=ffn1_post_g[l], ffn1_wg=ffn1_wg[l],
                 ffn1_wu=ffn1_wu[l], ffn1_wd=ffn1_wd[l], mix_pre_g=mix_pre_g[l], mix_post_g=mix_post_g[l],
                 w_in=w_in[l], ssd_conv_w=ssd_conv_w[l], ssd_conv_b=ssd_conv_b[l],
                 ssd_dt_bias=ssd_dt_bias[l], ssd_a_log=ssd_a_log[l], ssd_d=ssd_d[l],
                 ssd_norm_g=ssd_norm_g[l], lru_conv_w=lru_conv_w[l], lru_conv_b=lru_conv_b[l],
                 lru_wa=lru_wa[l], lru_ba=lru_ba[l], lru_wx=lru_wx[l], lru_bx=lru_bx[l],
                 lru_lambda=lru_lambda[l], w_out=w_out[l], ffn2_pre_g=ffn2_pre_g[l],
                 ffn2_post_g=ffn2_post_g[l], ffn2_wg=ffn2_wg[l], ffn2_wu=ffn2_wu[l], ffn2_wd=ffn2_wd[l])
        xp, h1, c1, h2, c2 = trunk_layer(
            xp,
            jnp.zeros((bp, SSD_HEADS, SSD_HEAD_DIM, SSD_STATE), jnp.float32),
            jnp.zeros((bp, CONV_W - 1, SSD_CONV_DIM), xp.dtype),
            jnp.zeros((bp, D_LRU), jnp.float32),
            jnp.zeros((bp, CONV_W - 1, D_LRU), xp.dtype),
            prompt_segments, p)
        p_ssd.append(h1); p_ssdc.append(c1); p_lru.append(h2); p_lruc.append(c2)
        xs, h1, c1, h2, c2 = trunk_layer(xs, state_ssd[l], state_ssd_conv[l], state_lru[l],
                                         state_lru_conv[l], sample_segments, p)
        s_ssd.append(h1); s_ssdc.append(c1); s_lru.append(h2); s_lruc.append(c2)
    y_prompt = xp[:, N_META:]
    return (y_prompt, xs,
            jnp.stack(p_ssd), jnp.stack(p_ssdc), jnp.stack(p_lru), jnp.stack(p_lruc),
            jnp.stack(s_ssd), jnp.stack(s_ssdc), jnp.stack(s_lru), jnp.stack(s_lruc))
```

```python
import os
from contextlib import ExitStack
import numpy as np
import concourse.bass as bass
import concourse.mybir as mybir
from concourse.bass_utils import run_bass_kernel_spmd

F32 = mybir.dt.float32
BF16 = mybir.dt.bfloat16


def _opt(name, default=None):
    return default

AF = mybir.ActivationFunctionType
ALU = mybir.AluOpType

D = 1024
DFF = 2816
NJ = DFF // 128
SEQ = 2048
NMETA = 16
NSAMP = 16
DEC = 4
EPS = 1e-6
NTOK = NMETA + NSAMP * DEC + SEQ
STILE = NMETA + NSAMP * DEC
TSPLIT = int(_opt("KTSPLIT", "1"))
if TSPLIT == 2:
    GROUPS = [
        (0, [("S", 0, STILE), ("T", STILE, 256), ("T", STILE + 256, 256)]),
        (STILE + 512, [("T", 0, 256), ("T", 256, 256)]),
        (STILE + 1024, [("T", 0, 256), ("T", 256, 256)]),
        (STILE + 1536, [("T", 0, 256), ("T", 256, 256)]),
    ]
else:
    GROUPS = [
        (0, [("S", 0, STILE), ("T", STILE, 512)]),
        (STILE + 512, [("T", 0, 512)]),
        (STILE + 1024, [("T", 0, 512)]),
        (STILE + 1536, [("T", 0, 512)]),
    ]
NTMAX = STILE + 512
NPV = 220
ENGS = ["pe", "act", "dve", "pool", "sp"]
SEG_EDGES = [0, STILE, 256, STILE + 256, 512, NTMAX]
CUR_TILES = [None]
PHASE = ["setup"]


class Buf:
    __slots__ = ("name", "w", "r")

    def __init__(self, name):
        self.name = name
        self.w = None
        self.r = {}


class Op:
    __slots__ = ("eng", "fn", "deps", "pos", "signal", "sigval", "dma", "dsem", "dval", "gid", "tag", "t0", "t1")


class TT:
    def __init__(self, name, handle, psum=False, tok=False):
        self.name = name
        self.h = handle
        self.bufs = {}
        self.psum = psum
        self.tok = tok

    def b(self, key=0):
        if self.psum:
            key = 0
        if self.tok:
            c, ti = key
            if ti == "all":
                segs = range(len(SEG_EDGES) - 1)
            else:
                if isinstance(ti, tuple):
                    lo_, hi_ = ti
                else:
                    kind, off, n = CUR_TILES[0][ti]
                    lo_, hi_ = off, off + n
                segs = [i for i in range(len(SEG_EDGES) - 1) if SEG_EDGES[i] < hi_ and SEG_EDGES[i + 1] > lo_]
            return [self._b((c, sg_)) for sg_ in segs]
        return self._b(key)

    def _b(self, key):
        v = self.bufs.get(key)
        if v is None:
            v = self.bufs[key] = Buf(f"{self.name}:{key}")
        return v

    def __getitem__(self, idx):
        return self.h[idx]


class View:
    def __init__(self, ap, bufs):
        self.ap = ap
        self.bufs = bufs

    def b(self, key=0):
        return self.bufs

    def __getitem__(self, idx):
        return self.ap[idx]


class Sched:
    def __init__(self):
        self.ops = {e: [] for e in ENGS}
        self.n = 0

    def add(self, eng, fn, reads=(), writes=(), dma=False):
        op = Op()
        op.eng, op.fn, op.dma = eng, fn, dma
        op.signal, op.sigval, op.dsem, op.dval = False, 0, None, 0
        op.gid = self.n
        op.tag = PHASE[0]
        self.n += 1
        def _flat(lst):
            out_ = []
            for b_ in lst:
                if isinstance(b_, (list, tuple)):
                    out_.extend(_flat(b_))
                else:
                    out_.append(b_)
            return out_
        reads, writes = _flat(reads), _flat(writes)
        deps = {}
        ex = [b for b in reads if b.name.startswith("ps")]
        if ex:
            reads = [b for b in reads if not b.name.startswith("ps")]
            writes = list(writes) + ex
        for b in reads:
            if b.w is not None:
                deps[b.w.gid] = b.w
        for b in writes:
            if b.w is not None:
                deps[b.w.gid] = b.w
            for k, r in b.r.items():
                if isinstance(r, list):
                    for rr in r:
                        deps[rr.gid] = rr
                else:
                    deps[r.gid] = r
        op.deps = list(deps.values())
        for b in writes:
            b.w = op
            b.r = {}
        for b in reads:
            b.r.setdefault("all", []).append(op)
        op.pos = len(self.ops[eng])
        self.ops[eng].append(op)
        return op


S = None


def PE(name, *a, r=(), w=(), **k):
    return S.add("pe", (name, a, k), r, w)


def ACT(name, *a, r=(), w=(), **k):
    return S.add("act", (name, a, k), r, w)


def DVE(name, *a, r=(), w=(), **k):
    return S.add("dve", (name, a, k), r, w)


def POOL(name, *a, r=(), w=(), **k):
    return S.add("pool", (name, a, k), r, w)


def DMA(q, out, in_, r=(), w=()):
    return S.add(q, ("dma_start", (), dict(out=out, in_=in_)), r, w, dma=True)


PE_FIX = float(_opt("KPEFIX", "0.05"))
PE_RATE = float(_opt("KPERATE", "2000"))
ENG_SCALE = {"act": 1.0, "dve": 1.0, "pool": 1.0, "dma": 1.0}


def _op_cost(op):
    name, a, k = op.fn
    out = k.get("out", a[0] if a else None)
    try:
        shp = list(out.shape)
    except Exception:
        shp = [128, 64]
    free = 1
    for d in shp[1:]:
        free *= int(d)
    if op.dma:
        return 0.15 if op.eng == "sp" else 0.65, (2.2 + free * shp[0] * 4 / 200e3) * ENG_SCALE["dma"]
    if op.eng == "pe":
        f32 = False
        try:
            f32 = (k.get("lhsT", a[1] if len(a) > 1 else None).dtype == F32) and name == "matmul"
        except Exception:
            pass
        t = PE_FIX + max(free, 64) / PE_RATE * (4 if f32 else 1)
        return t, t + 0.15
    if op.eng == "act":
        t = (0.2 + free / 1100.0) * ENG_SCALE["act"]
        return t, t + 0.05
    if op.eng == "dve":
        t = (0.1 + free / 900.0 * (2 if name == "tensor_tensor_scan" else 1)) * ENG_SCALE["dve"]
        return t, t + 0.05
    t = (0.15 + free / 450.0) * ENG_SCALE["pool"]
    return t, t + 0.05


def list_schedule(sched):
    import heapq
    allops = []
    for e in ENGS:
        allops.extend(sched.ops[e])
    succ = {op.gid: [] for op in allops}
    indeg = {}
    for op in allops:
        indeg[op.gid] = len(op.deps)
        for d in op.deps:
            succ[d.gid].append(op)
    ready_t = {op.gid: 0.0 for op in allops}
    rank = {}
    for op in sorted(allops, key=lambda o: -o.gid):
        lat_ = _op_cost(op)[1]
        rank[op.gid] = lat_ + max([rank[s_.gid] for s_ in succ[op.gid]], default=0.0)
    use_rank = _opt("KRANK", "0") == "1"
    fin = {}
    heaps = {e: [] for e in ENGS}
    for op in allops:
        if indeg[op.gid] == 0:
            heapq.heappush(heaps[op.eng], (0.0, op.gid, op))
    t_eng = {e: 0.0 for e in ENGS}
    new = {e: [] for e in ENGS}
    remaining = len(allops)
    while remaining:
        best = None
        for e in ENGS:
            h = heaps[e]
            if not h:
                continue
            rt, gid, op = h[0]
            st = max(rt, t_eng[e])
            key = (st, gid)
            if best is None or key < best[0]:
                best = (key, e)
        (st, _), e = best
        h = heaps[e]
        cands = []
        while h and h[0][0] <= st + 1e-9:
            cands.append(heapq.heappop(h))
        if use_rank:
            cands.sort(key=lambda c: (-rank[c[1]], c[1]))
        else:
            cands.sort(key=lambda c: c[1])
        rt, gid, op = cands[0]
        for c in cands[1:]:
            heapq.heappush(h, c)
        occ, lat = _op_cost(op)
        t_eng[e] = st + occ
        fin[gid] = st + lat
        op.t0, op.t1 = st, st + occ
        new[e].append(op)
        remaining -= 1
        for sop in succ[gid]:
            if fin[gid] > ready_t[sop.gid]:
                ready_t[sop.gid] = fin[gid]
            indeg[sop.gid] -= 1
            if indeg[sop.gid] == 0:
                heapq.heappush(heaps[sop.eng], (ready_t[sop.gid], sop.gid, sop))
    for e in ENGS:
        sched.ops[e] = new[e]
        for i, op in enumerate(new[e]):
            op.pos = i
    return max(t_eng.values())


def emit(nc, stack, sched):
    if _opt("KNOSCHED", "0") != "1":
        est = list_schedule(sched)
        if _opt("KVERBOSE"):
            print("list-schedule estimate (us):", est)
    engobj = {"pe": "tensor", "act": "scalar", "dve": "vector", "pool": "gpsimd", "sp": "sync"}
    esem = {e: stack.enter_context(nc.semaphore(f"s_{e}")) for e in ENGS}
    NDS = {"sp": 24, "pool": 24, "act": 4, "pe": 1, "dve": 1}
    dsems = {e: [stack.enter_context(nc.semaphore(f"d_{e}{i}")) for i in range(NDS[e])] for e in ENGS}
    for e in ENGS:
        for op in sched.ops[e]:
            for d in op.deps:
                if d.dma:
                    continue
                if d.eng != op.eng:
                    d.signal = True
                elif op.dma:
                    d.signal = True
                elif d.eng != "pe":
                    d.signal = True
    dcount = {e: 0 for e in ENGS}
    dprev = {}
    final_d = {}
    for e in ENGS:
        c = 0
        for op in sched.ops[e]:
            if op.dma:
                k = dcount[e]
                dcount[e] += 1
                s = dsems[e][k % NDS[e]]
                op.dsem = s
                op.dval = 16 * (k // NDS[e] + 1)
                final_d[(e, k % NDS[e])] = (s, op.dval)
            elif op.signal:
                c += 1
                op.sigval = c

    def run(e, eng):
        waited = {}

        def wait(sem, val):
            key = id(sem)
            if waited.get(key, 0) < val:
                eng.wait_ge(sem, val)
                waited[key] = val

        for op in sched.ops[e]:
            for d in op.deps:
                if d.dma:
                    wait(d.dsem, d.dval)
                elif d.eng != op.eng or op.dma or d.eng != "pe":
                    wait(esem[d.eng], d.sigval)
            if op.dma and op.dval > 16:
                wait(op.dsem, op.dval - 16)
            nm_, a_, k_ = op.fn
            ins = getattr(eng, nm_)(*a_, **k_)
            if op.dma:
                ins.then_inc(op.dsem, 16)
            elif op.signal:
                ins.then_inc(esem[e], 1)
        if e == "sp":
            for (s, v) in final_d.values():
                wait(s, v)

    with nc.Block() as block:
        @block.tensor
        def _(t):
            run("pe", t)

        @block.scalar
        def _(t):
            run("act", t)

        @block.vector
        def _(t):
            run("dve", t)

        @block.gpsimd
        def _(t):
            run("pool", t)

        @block.sync
        def _(t):
            run("sp", t)


def build_program(stop_after=None):
    global S
    S = Sched()
    nc = bass.Bass("TRN2", target_bir_lowering=False)
    stack = ExitStack()

    def din(name, shape):
        return nc.dram_tensor(name, list(shape), F32, kind="ExternalInput").ap()

    def dout(name, shape):
        return nc.dram_tensor(name, list(shape), F32, kind="ExternalOutput").ap()

    xT_in = din("xT_in", [D, NTOK])
    st_ssd = din("st_ssd", [2, NSAMP, D, 128])
    st_ssdc = din("st_ssdc", [2, 128, 12, NSAMP, 3])
    st_lru = din("st_lru", [2, 128, 8, NSAMP])
    st_lruc = din("st_lruc", [2, 128, 8, NSAMP, 3])
    cst_in = din("cst", [128, 512])
    cst2_in = din("cst2", [128, 384])
    pv_in = din("pv", [2, 128, NPV])
    w_gu = din("w_gu", [2, 2, NJ, 128, 2048])
    w_d = din("w_d", [2, 2, 8, 128, DFF])
    w_zx = din("w_zx", [2, 20, 128, 1024])
    w_dt = din("w_dt", [2, 128, 128])
    w_lru = din("w_lru", [2, 8, 128, 2048])
    w_ab = din("w_ab", [2, 8, 128, 256])
    w_out = din("w_out", [2, 8, 128, 2048])

    yT_out = dout("yT_out", [D, NTOK])
    o_ssd_p = dout("o_ssd_p", [2, 128, D])
    o_ssdc_p = dout("o_ssdc_p", [2, 128, 12, 3])
    o_lru_p = dout("o_lru_p", [2, 128, 8])
    o_lruc_p = dout("o_lruc_p", [2, 128, 8, 3])
    o_ssd_s = dout("o_ssd_s", [2, NSAMP, D, 128])
    o_ssdc_s = dout("o_ssdc_s", [2, 128, 12, NSAMP, 3])
    o_lru_s = dout("o_lru_s", [2, 128, 8, NSAMP])
    o_lruc_s = dout("o_lruc_s", [2, 128, 8, NSAMP, 3])

    def sb(name, shape, dt=F32, tok=False):
        return TT(name, stack.enter_context(nc.sbuf_tensor("sb_" + name, list(shape), dt)), tok=tok)

    x = sb("x", [128, 8, NTMAX], tok=True)
    xn = sb("xn", [128, 8, NTMAX], BF16, tok=True)
    big = sb("big", [128, 24, NTMAX], BF16, tok=True)
    o = sb("o", [128, 8, NTMAX], tok=True)
    xbcT = sb("xbcT", [128, 12, NTMAX], BF16, tok=True)
    cst = sb("cst", [128, 512])
    cst2 = sb("cst2", [128, 384])
    identb = sb("identb", [128, 128], BF16)
    onesb = sb("onesb", [128, 128], BF16)
    pv = sb("pvt", [128, 2, NPV])
    pd = sb("pdt", [128, 2, 96])
    wslots = [sb(f"ws{i}", [128, DFF], BF16) for i in range(3)]
    wdt = sb("wdt", [128, 2, 128], BF16)
    wab = [sb(f"wab{i}", [128, 256], BF16) for i in range(2)]
    hT_p = sb("hT_p", [128, 2, D])
    tail_xbc = sb("tail_xbc", [128, 2, 12, 3])
    tail_xr = sb("tail_xr", [128, 2, 8, 3])
    lru_carry = sb("lru_carry", [128, 2, 8])
    sin_ssdc = sb("sin_ssdc", [128, 12, NSAMP, 3])
    sin_lruc = sb("sin_lruc", [128, 8, NSAMP, 3])
    sin_lru = sb("sin_lru", [128, 8, NSAMP])
    sout_ssdc, sout_lruc = sin_ssdc, sin_lruc
    sout_lru = sb("sout_lru", [128, 8, NSAMP])
    rs = [sb(f"rs{i}", [128, 512]) for i in range(1)]
    rr = [sb(f"rr{i}", [128, 512]) for i in range(2)]
    sg = [sb(f"sg{i}", [128, 512]) for i in range(2)]
    tmpn = sg
    xp = [sb(f"xp{i}", [128, 515]) for i in range(2)]
    xps = [sb(f"xps{i}", [128, NSAMP, 7]) for i in range(2)]
    acc = [sb(f"acc{i}", [128, 512]) for i in range(2)]
    lt = {k: sb(f"lt_{k}", [128, 512]) for k in ["r", "ig", "a", "a2", "hs", "g2"]}
    lt["mu"], lt["bt"], lt["gl"] = lt["a2"], lt["ig"], lt["g2"]
    xrbf = sb("xrbf", [128, 512], BF16)
    tmp16 = sb("tmp16", [128, NSAMP])
    R1 = sb("R1", [128, 2048])
    Lb = sb("Lb", [128, 2048], BF16)
    scT = sb("scT", [128, 2048], BF16)
    cbm = sb("cbm", [128, 256])
    xdt = sb("xdt", [128, D], BF16)
    xw = sb("xw", [128, D], BF16)
    Btok = sb("Btok", [128, 256], BF16)
    ydsb = sb("ydsb", [128, D])
    ytok = sb("ytok", [128, D])
    sm = sb("sm", [128, 128])
    hTs = sb("hTs", [128, D])
    hTbf = sb("hTbf", [128, D], BF16)
    nat = [sb(f"nat{i}", [128, 8, 128]) for i in range(2)]
    BtokM = sb("BtokM", [128, 256], BF16)
    nato = nat
    sqg = sb("sqg", [128, D], BF16)
    rsg = sb("rsg", [128, 128])
    rrg = sb("rrg", [128, 128])

    oflat = o.h[:, :, :].rearrange("p k n -> p (k n)")
    oall = [o.b((k, "all")) for k in range(8)]
    g1o = View(oflat[:, 0:1024], oall[0:2])
    ygo = View(oflat[:, 2 * NTMAX:2 * NTMAX + 1024], oall[2:4])
    ps = [TT(f"ps{i}", stack.enter_context(nc.psum_tensor(f"ps{i}", [128, 512], F32)), psum=True) for i in range(8)]

    ident = lambda T=128: cst[0:T, 0:T]
    tri = lambda T=128: cst[0:T, 128:128 + T]
    Um = lambda T=128: cst[0:T, 256:256 + T]
    onesf = lambda T=128: cst[0:T, 384:512]

    def pvc(l, lo, hi=None):
        return pv[:, l, lo:(hi if hi is not None else lo + 1)]

    DMA("sp", cst[:, :], cst_in[:, :], w=[cst.b()])
    DMA("sp", cst2[:, :], cst2_in[:, :], w=[cst2.b()])
    DMA("sp", pv[:, :, :], pv_in.rearrange("l p n -> p l n"), w=[pv.b()])
    ACT("activation", out=identb[:, :], in_=cst[:, 0:128], func=AF.Copy, r=[cst.b()], w=[identb.b()])
    ACT("activation", out=onesb[:, :], in_=cst[:, 384:512], func=AF.Copy, r=[cst.b()], w=[onesb.b()])
    for l in range(2):
        ACT("activation", out=pd[:, l, 56:64], in_=pv[:, l, 80:88], func=AF.Exp, scale=-1.0, r=[pv.b()], w=[pd.b()])
        ACT("activation", out=pd[:, l, 56:64], in_=pd[:, l, 56:64], func=AF.Ln, bias=1.0, scale=1.0, r=[pd.b()], w=[pd.b()])
        ACT("activation", out=pd[:, l, 0:16], in_=pv[:, l, 204:220], func=AF.Exp, r=[pv.b(), pd.b()], w=[pd.b()])
        DVE("tensor_scalar", pd[:, l, 16:24], pd[:, l, 56:64], -8.0, None, ALU.mult, r=[pd.b()], w=[pd.b()])
        DVE("tensor_scalar", pd[:, l, 24:32], pd[:, l, 56:64], -16.0, None, ALU.mult, r=[pd.b()], w=[pd.b()])
        DVE("tensor_scalar", pd[:, l, 0:16], pd[:, l, 0:16], -1.0, None, ALU.mult, r=[pd.b()], w=[pd.b()])
        DVE("tensor_scalar", pd[:, l, 32:40], pv[:, l, 8:16], 0.5, None, ALU.mult, r=[pv.b(), pd.b()], w=[pd.b()])
        DVE("tensor_scalar", pd[:, l, 40:48], pv[:, l, 40:48], 0.5, None, ALU.mult, r=[pv.b(), pd.b()], w=[pd.b()])
        DVE("tensor_scalar", pd[:, l, 64:80], pv[:, l, 64:80], -1.0, None, ALU.mult, r=[pv.b(), pd.b()], w=[pd.b()])
        POOL("memset", lru_carry[:, l, :], 0.0, w=[lru_carry.b(l)])

    cnt = {"ws": 0, "bank": 0, "rs": 0, "sg": 0, "xp": 0, "wab": 0, "nat": 0, "bankd": 0, "nb": 0}

    wplan = []
    wstate = {"issued": 0, "used": 0}

    def w_issue(i):
        ncols, src, split = wplan[i]
        wt = wslots[i % 3]
        if split:
            DMA("pool", wt[:, 0:ncols].rearrange("p (a b) -> p a b", a=2), src.rearrange("p (a b) -> p a b", a=2), w=[wt.b()])
        else:
            DMA("pool", wt[:, 0:ncols], src, w=[wt.b()])

    def next_ws():
        u = wstate["used"]
        while wstate["issued"] < min(len(wplan), u + 3):
            w_issue(wstate["issued"])
            wstate["issued"] += 1
        wstate["used"] = u + 1
        return wslots[u % 3]

    def plan_ffn(l, f):
        for j in range(NJ):
            wplan.append((2048, w_gu[l, f, j], False))
        for m in range(8):
            wplan.append((DFF, w_d[l, f, m], True))

    def plan_mixer(l):
        for blk in range(20):
            wplan.append((1024, w_zx[l, blk], False))
        for c in range(8):
            wplan.append((2048, w_lru[l, c], False))
        for m in range(8):
            wplan.append((2048, w_out[l, m], False))

    def rot(key, lst):
        i = cnt[key] % len(lst)
        cnt[key] += 1
        return lst[i]

    def rms_scale(pbank, n):
        i = cnt["rs"] % 4
        cnt["rs"] += 1
        a, b = rs[0], rr[i // 2]
        co = (i % 2) * 256
        ACT("activation", out=a[:, co:co + n], in_=pbank[:, 0:n], func=AF.Ln, bias=EPS, scale=1.0 / D, r=[pbank.b()], w=[a.b(i % 2)])
        ACT("activation", out=b[:, co:co + n], in_=a[:, co:co + n], func=AF.Exp, scale=-0.5, r=[a.b(i % 2)], w=[b.b(i % 2)])
        return b, co, i % 2

    def subtiles(off, n):
        if n <= 256:
            return [(off, n)]
        h = n // 2
        return [(off, h), (off + h, n - h)]

    def prenorm(tiles, gcol):
        for ti, (kind, off0, n0) in enumerate(tiles):
          for (off, n) in subtiles(off0, n0):
            rg = (off, off + n)
            pbk = ps[6 + (cnt["nb"] % 2)]
            cnt["nb"] += 1
            for kc in range(8):
                ACT("activation", out=xbcT[:, kc, off:off + n], in_=x[:, kc, off:off + n], func=AF.Square,
                    r=[x.b((kc, rg))], w=[xbcT.b((kc, rg))])
            for kc in range(8):
                PE("matmul", pbk[:, 0:n], lhsT=onesb[:, :], rhs=xbcT[:, kc, off:off + n], start=(kc == 0), stop=(kc == 7),
                   r=[onesb.b(), xbcT.b((kc, rg))], w=[pbk.b()])
            rt, co, rk = rms_scale(pbk, n)
            for kc in range(8):
                DVE("scalar_tensor_tensor", out=xn[:, kc, off:off + n], in0=x[:, kc, off:off + n],
                                                                    scalar=gcol[:, kc:kc + 1], in1=rt[:, co:co + n],
                                                                    op0=ALU.mult, op1=ALU.mult,
                    r=[x.b((kc, rg)), rt.b(rk), pv.b(), pd.b()], w=[xn.b((kc, rg))])

    def down_postnorm(tiles, src, KC, wsrc_fn, wrow, gcol):
        for m in range(8):
            wt = next_ws()
            for ti, (kind, off, n) in enumerate(tiles):
                pb = ps[4 + (cnt["bankd"] % 2)]
                cnt["bankd"] += 1
                for k in range(KC):
                    PE("matmul", pb[:, 0:n], lhsT=wt[:, k * 128:(k + 1) * 128], rhs=src[:, k, off:off + n],
                                                             start=(k == 0), stop=(k == KC - 1),
                       r=[wt.b(), big.b((k, ti))], w=[pb.b()])
                ACT("activation", out=o[:, m, off:off + n], in_=pb[:, 0:n], func=AF.Copy,
                    r=[pb.b()], w=[o.b((m, ti))])
                ACT("activation", out=xn[:, m, off:off + n], in_=pb[:, 0:n], func=AF.Square,
                    r=[pb.b()], w=[xn.b((m, ti))])
        for ti, (kind, off0, n0) in enumerate(tiles):
          for (off, n) in subtiles(off0, n0):
            rg = (off, off + n)
            pbk = ps[6 + (cnt["nb"] % 2)]
            cnt["nb"] += 1
            for m in range(8):
                PE("matmul", pbk[:, 0:n], lhsT=onesb[:, :], rhs=xn[:, m, off:off + n], start=(m == 0), stop=(m == 7),
                   r=[onesb.b(), xn.b((m, rg))], w=[pbk.b()])
            rt, co, rk = rms_scale(pbk, n)
            for m in range(8):
                tm = rot("sg", tmpn)
                DVE("scalar_tensor_tensor", out=tm[:, 0:n], in0=o[:, m, off:off + n],
                                                                        scalar=gcol[:, m:m + 1], in1=rt[:, co:co + n],
                                                                        op0=ALU.mult, op1=ALU.mult,
                    r=[o.b((m, rg)), rt.b(rk), pv.b(), pd.b()], w=[tm.b()])
                eng_add = POOL if (m % 2 == 0) else DVE
                eng_add("tensor_tensor", out=x[:, m, off:off + n], in0=x[:, m, off:off + n], in1=tm[:, 0:n], op=ALU.add,
                        r=[x.b((m, rg)), tm.b()], w=[x.b((m, rg))])

    def ffn(l, f, tiles):
        gpre = pvc(l, 0, 8) if f == 0 else pvc(l, 32, 40)
        gpost = pd[:, l, 32:40] if f == 0 else pd[:, l, 40:48]
        prenorm(tiles, gpre)
        for j in range(NJ):
            wt = next_ws()
            for ti, (kind, off, n) in enumerate(tiles):
                bi = (cnt["bank"] % 2) * 2
                cnt["bank"] += 1
                pg, pu = ps[bi], ps[bi + 1]
                for kc in range(8):
                    PE("matmul", pg[:, 0:n], lhsT=wt[:, kc * 256:kc * 256 + 128], rhs=xn[:, kc, off:off + n],
                                                               start=(kc == 0), stop=(kc == 7),
                       r=[wt.b(), xn.b((kc, ti))], w=[pg.b()])
                for kc in range(8):
                    PE("matmul", pu[:, 0:n], lhsT=wt[:, kc * 256 + 128:kc * 256 + 256], rhs=xn[:, kc, off:off + n],
                                                               start=(kc == 0), stop=(kc == 7),
                       r=[wt.b(), xn.b((kc, ti))], w=[pu.b()])
                st = rot("sg", sg)
                ACT("activation", out=st[:, 0:n], in_=pg[:, 0:n], func=AF.Silu, r=[pg.b()], w=[st.b()])
                DVE("tensor_tensor", out=big[:, j, off:off + n], in0=st[:, 0:n], in1=pu[:, 0:n], op=ALU.mult,
                    r=[st.b(), pu.b()], w=[big.b((j, ti))])
        down_postnorm(tiles, big, NJ, lambda m: w_d[l, f, m], DFF, gpost)

    def conv(kind, n, pp, wcol, bcol, tail, tailb, sin, sinb, sout, soutb, out_fn, out_bufs, silu):
        ci = cnt["xp"] % 2
        cnt["xp"] += 1
        ac, xt, xs_ = acc[ci], xp[ci], xps[ci]
        if kind == "T":
            ACT("activation", out=xt[:, 0:3], in_=tail, func=AF.Copy, r=[tailb], w=[xt.b()])
            ACT("activation", out=xt[:, 3:3 + n], in_=pp[:, 0:n], func=AF.Copy, r=[pp.b()], w=[xt.b()])
            DVE("tensor_copy", tail, xt[:, n:n + 3], r=[xt.b()], w=[tailb])
            DVE("tensor_scalar", ac[:, 0:n], xt[:, 3:3 + n], wcol[:, 3:4], bcol, ALU.mult, ALU.add, r=[xt.b(), pv.b()], w=[ac.b()])
            for k in (2, 1, 0):
                DVE("scalar_tensor_tensor", out=ac[:, 0:n], in0=xt[:, k:k + n], scalar=wcol[:, k:k + 1], in1=ac[:, 0:n],
                                                          op0=ALU.mult, op1=ALU.add, r=[xt.b(), ac.b(), pv.b()], w=[ac.b()])
        else:
            nm = NMETA
            DVE("memset", xt[:, 0:3], 0.0, w=[xt.b()])
            ACT("activation", out=xt[:, 3:3 + nm], in_=pp[:, 0:nm], func=AF.Copy, r=[pp.b()], w=[xt.b()])
            DVE("tensor_copy", tail, xt[:, nm:nm + 3], r=[xt.b()], w=[tailb])
            DVE("tensor_scalar", ac[:, 0:nm], xt[:, 3:3 + nm], wcol[:, 3:4], bcol, ALU.mult, ALU.add, r=[xt.b(), pv.b()], w=[ac.b()])
            for k in (2, 1, 0):
                DVE("scalar_tensor_tensor", out=ac[:, 0:nm], in0=xt[:, k:k + nm], scalar=wcol[:, k:k + 1], in1=ac[:, 0:nm],
                                                          op0=ALU.mult, op1=ALU.add, r=[xt.b(), ac.b(), pv.b()], w=[ac.b()])
            ACT("activation", out=xs_[:, :, 0:3], in_=sin, func=AF.Copy, r=[sinb], w=[xs_.b()])
            ACT("activation", out=xs_[:, :, 3:7], in_=pp[:, nm:n].rearrange("p (s t) -> p s t", t=DEC), func=AF.Copy,
                r=[pp.b()], w=[xs_.b()])
            DVE("tensor_copy", sout, xs_[:, :, 4:7], r=[xs_.b()], w=[soutb])
            acs = ac[:, nm:n].rearrange("p (s t) -> p s t", t=DEC)
            DVE("tensor_scalar", acs, xs_[:, :, 3:7], wcol[:, 3:4], bcol, ALU.mult, ALU.add, r=[xs_.b(), pv.b()], w=[ac.b()])
            for k in (2, 1, 0):
                DVE("scalar_tensor_tensor", out=acs, in0=xs_[:, :, k:k + DEC], scalar=wcol[:, k:k + 1], in1=acs,
                                                          op0=ALU.mult, op1=ALU.add, r=[xs_.b(), ac.b(), pv.b()], w=[ac.b()])
        if silu:
            ACT("activation", out=out_fn(), in_=ac[:, 0:n], func=AF.Silu, r=[ac.b()], w=out_bufs)
        return ac

    def ssd_chunk(l, ti, col0, T, init, hT, hTb):
        hTt, hTap = hT
        cols = slice(col0, col0 + T)
        xsb = [xbcT.b((c, ti)) for c in range(12)]
        p7 = ps[7]
        for kc in range(8):
            PE("matmul", p7[0:T, 0:16], lhsT=xn[:, kc, cols], rhs=wdt[:, l, kc * 16:(kc + 1) * 16],
                                         start=(kc == 0), stop=(kc == 7),
               r=[xn.b((kc, ti)), wdt.b()], w=[p7.b("a")])
        DVE("tensor_tensor", out=sm[0:T, 0:16], in0=p7[0:T, 0:16], in1=pv[0:T, l, 188:204], op=ALU.add,
            r=[p7.b("a"), pv.b()], w=[sm.b("t1")])
        ACT("activation", out=sm[0:T, 16:32], in_=sm[0:T, 0:16], func=AF.Exp, r=[sm.b("t1")], w=[sm.b("e1")])
        ACT("activation", out=sm[0:T, 32:48], in_=sm[0:T, 16:32], func=AF.Ln, bias=1.0, scale=1.0, r=[sm.b("e1")], w=[sm.b("dt")])
        DVE("tensor_tensor", out=sm[0:T, 48:64], in0=sm[0:T, 32:48], in1=pd[0:T, l, 0:16], op=ALU.mult,
            r=[sm.b("dt"), pd.b()], w=[sm.b("dta")])
        yield
        p6b = ps[6][:, :].bitcast(BF16)
        for c in range(8):
            PE("transpose", p6b[0:T, c * 128:(c + 1) * 128], xbcT[:, c, cols], identb[:, :],
               r=[xsb[c], identb.b()], w=[ps[6].b()])
        p7b = p7[:, 320:448].bitcast(BF16)
        for g in range(2):
            PE("transpose", p7b[0:T, g * 128:(g + 1) * 128], xbcT[:, 8 + g, cols], identb[:, :],
               r=[xsb[8 + g], identb.b()], w=[p7.b("bt")])
        dtb = sm[0:T, 32:48].unsqueeze(2).to_broadcast([T, 16, 64])
        DVE("tensor_tensor", out=xdt[0:T, :].rearrange("p (h d) -> p h d", d=64),
                                      in0=p6b[0:T, :].rearrange("p (h d) -> p h d", d=64), in1=dtb, op=ALU.mult,
            r=[ps[6].b(), sm.b("dt")], w=[xdt.b()])
        ACT("activation", out=Btok[0:T, :], in_=p7b[0:T, :], func=AF.Copy, r=[p7.b("bt")], w=[Btok.b()])
        yield
        R1v = R1[0:T, 0:16 * T].rearrange("p (h i) -> p h i", i=T)
        EA = DVE if "a" in _opt("KDVE", "") else POOL
        EB = DVE if "b" in _opt("KDVE", "") else POOL
        EC = DVE if "c" in _opt("KDVE", "") else POOL
        ED = DVE if "d" in _opt("KDVE", "") else POOL
        if _opt("KR1SPLIT", "0") == "1" and T == 128:
            for hh, E_ in ((0, DVE), (1, POOL)):
                E_("tensor_tensor", out=R1v[:, hh * 8:(hh + 1) * 8, :], in0=tri(T).unsqueeze(1).to_broadcast([T, 8, T]),
                   in1=sm[0:T, 48 + hh * 8:56 + hh * 8].unsqueeze(2).to_broadcast([T, 8, T]), op=ALU.mult,
                   r=[cst.b(), sm.b("dta")], w=[R1.b(hh)])
            R1r = [R1.b(0), R1.b(1), R1.b()]
        else:
            EA("tensor_tensor", out=R1v, in0=tri(T).unsqueeze(1).to_broadcast([T, 16, T]),
                                      in1=sm[0:T, 48:64].unsqueeze(2).to_broadcast([T, 16, T]), op=ALU.mult,
               r=[cst.b(), sm.b("dta")], w=[R1.b()])
            R1r = [R1.b()]
        yield
        nq = (16 * T + 511) // 512
        PE("matmul", p7[0:T, 16:32], lhsT=Um(T), rhs=sm[0:T, 48:64], start=True, stop=True, r=[cst.b(), sm.b("dta")], w=[p7.b("b")])
        PE("matmul", p7[0:T, 32:48], lhsT=tri(T), rhs=sm[0:T, 48:64], start=True, stop=True, r=[cst.b(), sm.b("dta")], w=[p7.b("c")])
        PE("matmul", p7[:, 48:64], lhsT=onesf(T), rhs=sm[0:T, 48:64], start=True, stop=True, r=[cst.b(), sm.b("dta")], w=[p7.b("d")])
        for g in range(2):
            PE("matmul", p7[0:T, 64 + g * 128:64 + g * 128 + T], lhsT=xbcT[:, 8 + g, cols], rhs=xbcT[:, 10 + g, cols],
                                       start=True, stop=True,
               r=[xsb[8 + g], xsb[10 + g]], w=[p7.b("cb")])
        for half in range((nq + 1) // 2):
            qs = [q for q in (2 * half, 2 * half + 1) if q < nq]
            for q in qs:
                w_ = min(512, 16 * T - q * 512)
                PE("matmul", ps[q % 2][0:T, 0:w_], lhsT=Um(T), rhs=R1[0:T, q * 512:q * 512 + w_], start=True, stop=True,
                   r=[cst.b()] + R1r, w=[ps[q % 2].b()])
            for q in qs:
                w_ = min(512, 16 * T - q * 512)
                ACT("activation", out=Lb[0:T, q * 512:q * 512 + w_], in_=ps[q % 2][0:T, 0:w_], func=AF.Exp,
                    r=[ps[q % 2].b()], w=[Lb.b()])
        yield
        cbv = p7[0:T, 64:320].rearrange("p (g i) -> p g i", i=128)[:, :, 0:T]
        cbmv = cbm[0:T, 0:2 * T].rearrange("p (g i) -> p g i", i=T)
        DVE("tensor_tensor", out=cbmv, in0=cbv, in1=tri(T).unsqueeze(1).to_broadcast([T, 2, T]), op=ALU.mult,
            r=[p7.b("cb"), cst.b()], w=[cbm.b()])
        scv = scT[0:T, 0:16 * T].rearrange("p (g r i) -> p g r i", g=2, i=T)
        Lv = Lb[0:T, 0:16 * T].rearrange("p (g r i) -> p g r i", g=2, i=T)
        DVE("tensor_tensor", out=scv, in0=Lv, in1=cbmv.unsqueeze(2).to_broadcast([T, 2, 8, T]), op=ALU.mult,
            r=[Lb.b(), cbm.b()], w=[scT.b()])
        ACT("activation", out=sm[0:T, 64:80], in_=p7[0:T, 16:32], func=AF.Exp, r=[p7.b("b")], w=[sm.b("wend")])
        ACT("activation", out=sm[0:T, 80:96], in_=p7[0:T, 32:48], func=AF.Exp, r=[p7.b("c")], w=[sm.b("ecum")])
        ACT("activation", out=sm[:, 96:112], in_=p7[:, 48:64], func=AF.Exp, r=[p7.b("d")], w=[sm.b("decay")])
        EB("tensor_tensor", out=xw[0:T, :].rearrange("p (h d) -> p h d", d=64),
                                      in0=xdt[0:T, :].rearrange("p (h d) -> p h d", d=64),
                                      in1=sm[0:T, 64:80].unsqueeze(2).to_broadcast([T, 16, 64]), op=ALU.mult,
            r=[xdt.b(), sm.b("wend")], w=[xw.b()])
        yield
        for h in range(16):
            pb = ps[h // 8]
            PE("matmul", pb[0:T, (h % 8) * 64:(h % 8) * 64 + 64], lhsT=scT[0:T, h * T:(h + 1) * T],
                                              rhs=xdt[0:T, h * 64:(h + 1) * 64], start=True, stop=True,
               r=[scT.b(), xdt.b()], w=[pb.b()])
        if init:
            for g in range(2):
                PE("matmul", ps[2 + g][0:T, :], lhsT=xbcT[:, 10 + g, cols], rhs=hTb[:, g * 512:(g + 1) * 512], start=True, stop=True,
                   r=[xsb[10 + g], hTbf.b()], w=[ps[2 + g].b()])
        yield
        ysrc = ydsb
        for g in range(2):
            ACT("activation", out=ydsb[0:T, g * 512:(g + 1) * 512], in_=ps[g][0:T, :], func=AF.Copy,
                r=[ps[g].b()], w=[ydsb.b(g)])
        if init:
            for g in range(2):
                DVE("tensor_tensor", out=ytok[0:T, g * 512:(g + 1) * 512].rearrange("p (h d) -> p h d", d=64),
                                                   in0=ps[2 + g][0:T, :].rearrange("p (h d) -> p h d", d=64),
                                                   in1=sm[0:T, 80 + g * 8:88 + g * 8].unsqueeze(2).to_broadcast([T, 8, 64]), op=ALU.mult,
                    r=[ps[2 + g].b(), sm.b("ecum")], w=[ytok.b(g)])
                DVE("tensor_tensor", out=ytok[0:T, g * 512:(g + 1) * 512], in0=ytok[0:T, g * 512:(g + 1) * 512],
                                                   in1=ydsb[0:T, g * 512:(g + 1) * 512], op=ALU.add,
                    r=[ytok.b(g), ydsb.b(g)], w=[ytok.b(g)])
            ysrc = ytok
        yield
        for c in range(8):
            pb = ps[(c * T) // 512]
            PE("transpose", pb[:, (c * T) % 512:(c * T) % 512 + T], ysrc[0:T, c * 128:(c + 1) * 128], ident(T),
               r=[ysrc.b(c // 4), cst.b()], w=[pb.b()])
        for g in range(2):
            PE("matmul", ps[2 + g][:, :], lhsT=Btok[0:T, g * 128:(g + 1) * 128], rhs=xw[0:T, g * 512:(g + 1) * 512],
                                       start=True, stop=True,
               r=[Btok.b(), xw.b()], w=[ps[2 + g].b()])
        yield
        if _opt("KGO", "0") == "1":
            g1, yg = g1o, ygo
            g1b, ygb = [g1o.b()], [ygo.b()]
        else:
            g1, yg = ydsb, ytok
            g1b, ygb = [ydsb.b(0), ydsb.b(1)], [ytok.b(0), ytok.b(1)]
        g1v = g1[:, 0:8 * T].rearrange("p (c t) -> p c t", t=T)
        ygv = yg[:, 0:8 * T].rearrange("p (c t) -> p c t", t=T)
        ED("tensor_tensor", out=g1v, in0=xbcT[:, 0:8, cols], in1=pv[:, l, 56:64].unsqueeze(2).to_broadcast([128, 8, T]), op=ALU.mult,
            r=xsb[0:8] + [pv.b()], w=g1b)
        nb = (8 * T + 511) // 512
        for q in range(nb):
            w_ = min(512, 8 * T - q * 512)
            DVE("tensor_tensor", out=g1[:, q * 512:q * 512 + w_], in0=g1[:, q * 512:q * 512 + w_],
                                                      in1=ps[q][:, 0:w_], op=ALU.add,
                r=g1b + [ps[q].b()], w=g1b)
        DVE("tensor_tensor", out=ygv, in0=g1v, in1=big[:, 16:24, cols], op=ALU.mult,
            r=g1b + [big.b((16 + c, ti)) for c in range(8)], w=ygb)
        ACT("activation", out=sqg[:, 0:8 * T], in_=yg[:, 0:8 * T], func=AF.Square, r=ygb, w=[sqg.b()])
        yield
        for c in range(8):
            PE("matmul", ps[6][:, 0:T], lhsT=onesb[:, :], rhs=sqg[:, c * T:(c + 1) * T], start=(c == 0), stop=(c == 7),
               r=[onesb.b(), sqg.b()], w=[ps[6].b()])
        ACT("activation", out=rsg[:, 0:T], in_=ps[6][:, 0:T], func=AF.Ln, bias=EPS, scale=1.0 / D, r=[ps[6].b()], w=[rsg.b()])
        ACT("activation", out=rrg[:, 0:T], in_=rsg[:, 0:T], func=AF.Exp, scale=-0.5, r=[rsg.b()], w=[rrg.b()])
        ED("tensor_tensor", out=ygv, in0=ygv, in1=pv[:, l, 48:56].unsqueeze(2).to_broadcast([128, 8, T]), op=ALU.mult,
            r=ygb + [pv.b()], w=ygb)
        DVE("tensor_tensor", out=big[:, 0:8, cols], in0=ygv, in1=rrg[:, 0:T].unsqueeze(1).to_broadcast([128, 8, T]), op=ALU.mult,
            r=ygb + [rrg.b()], w=[big.b((c, ti)) for c in range(8)])
        yield
        if init:
            EC("tensor_tensor", out=hTap.rearrange("p (h d) -> p h d", d=64), in0=hTap.rearrange("p (h d) -> p h d", d=64),
                                          in1=sm[:, 96:112].unsqueeze(2).to_broadcast([128, 16, 64]), op=ALU.mult,
                r=[hTt.b(l), sm.b("decay")], w=[hTt.b(l)])
            for g in range(2):
                DVE("tensor_tensor", out=hTap[:, g * 512:(g + 1) * 512], in0=hTap[:, g * 512:(g + 1) * 512],
                                                   in1=ps[2 + g][:, :], op=ALU.add,
                    r=[hTt.b(l), ps[2 + g].b()], w=[hTt.b(l)])
        else:
            for g in range(2):
                ACT("activation", out=hTap[:, g * 512:(g + 1) * 512], in_=ps[2 + g][:, :], func=AF.Copy,
                    r=[ps[2 + g].b()], w=[hTt.b(l)])
        ACT("activation", out=hTb[:, :], in_=hTap, func=AF.Copy, r=[hTt.b(l)], w=[hTbf.b()])
        yield

    def ssd_S(l, ti, off):
        T = STILE
        cols = slice(off, off + T)
        xsb = [xbcT.b((c, ti)) for c in range(12)]
        p7 = ps[7]
        triB = cst2[0:T, 0:T]
        UB = cst2[0:T, 128:128 + T]
        blk = cst2[0:T, 256:256 + 17]
        c2 = cst2.b()
        for kc in range(8):
            PE("matmul", p7[0:T, 0:16], lhsT=xn[:, kc, cols], rhs=wdt[:, l, kc * 16:(kc + 1) * 16], start=(kc == 0), stop=(kc == 7),
               r=[xn.b((kc, ti)), wdt.b()], w=[p7.b()])
        DVE("tensor_tensor", out=sm[0:T, 0:16], in0=p7[0:T, 0:16], in1=pv[0:T, l, 188:204], op=ALU.add, r=[p7.b(), pv.b()], w=[sm.b("t1")])
        ACT("activation", out=sm[0:T, 16:32], in_=sm[0:T, 0:16], func=AF.Exp, r=[sm.b("t1")], w=[sm.b("e1")])
        ACT("activation", out=sm[0:T, 32:48], in_=sm[0:T, 16:32], func=AF.Ln, bias=1.0, scale=1.0, r=[sm.b("e1")], w=[sm.b("dt")])
        DVE("tensor_tensor", out=sm[0:T, 48:64], in0=sm[0:T, 32:48], in1=pd[0:T, l, 0:16], op=ALU.mult, r=[sm.b("dt"), pd.b()], w=[sm.b("dta")])
        yield
        p6b = ps[6][:, :].bitcast(BF16)
        for c in range(8):
            PE("transpose", p6b[0:T, c * 128:(c + 1) * 128], xbcT[:, c, cols], identb[:, :], r=[xsb[c], identb.b()], w=[ps[6].b()])
        p7b = p7[:, 192:320].bitcast(BF16)
        for g in range(2):
            PE("transpose", p7b[0:T, g * 128:(g + 1) * 128], xbcT[:, 8 + g, cols], identb[:, :], r=[xsb[8 + g], identb.b()], w=[p7.b()])
        DVE("tensor_tensor", out=xdt[0:T, :].rearrange("p (h d) -> p h d", d=64), in0=p6b[0:T, :].rearrange("p (h d) -> p h d", d=64),
            in1=sm[0:T, 32:48].unsqueeze(2).to_broadcast([T, 16, 64]), op=ALU.mult, r=[ps[6].b(), sm.b("dt")], w=[xdt.b()])
        ACT("activation", out=Btok[0:T, :], in_=p7b[0:T, :], func=AF.Copy, r=[p7.b()], w=[Btok.b()])
        dtaE = ytok
        POOL("tensor_copy", ytok[0:T, :].rearrange("p (h d) -> p h d", d=64), sm[0:T, 48:64].unsqueeze(2).to_broadcast([T, 16, 64]),
             r=[sm.b("dta")], w=[ytok.b(0), ytok.b(1)])
        yield
        R1v = R1[0:T, 0:16 * T].rearrange("p (h i) -> p h i", i=T)
        POOL("tensor_tensor", out=R1v, in0=triB.unsqueeze(1).to_broadcast([T, 16, T]),
             in1=sm[0:T, 48:64].unsqueeze(2).to_broadcast([T, 16, T]), op=ALU.mult, r=[c2, sm.b("dta")], w=[R1.b()])
        yield
        PE("matmul", p7[0:T, 16:32], lhsT=UB, rhs=sm[0:T, 48:64], start=True, stop=True, r=[c2, sm.b("dta")], w=[p7.b()])
        for g in range(2):
            PE("matmul", p7[0:T, 32 + g * T:32 + (g + 1) * T], lhsT=xbcT[:, 8 + g, cols], rhs=xbcT[:, 10 + g, cols], start=True, stop=True,
               r=[xsb[8 + g], xsb[10 + g]], w=[p7.b()])
        for c in range(8):
            PE("matmul", p7[:, 320 + c * 17:320 + (c + 1) * 17], lhsT=ytok[0:T, c * 128:(c + 1) * 128], rhs=blk, start=True, stop=True,
               r=[ytok.b(c // 4), c2], w=[p7.b()])
        nq = (16 * T + 511) // 512
        for half in range((nq + 1) // 2):
            qs = [q for q in (2 * half, 2 * half + 1) if q < nq]
            for q in qs:
                w_ = min(512, 16 * T - q * 512)
                PE("matmul", ps[q % 2][0:T, 0:w_], lhsT=UB, rhs=R1[0:T, q * 512:q * 512 + w_], start=True, stop=True,
                   r=[c2, R1.b()], w=[ps[q % 2].b()])
            for q in qs:
                w_ = min(512, 16 * T - q * 512)
                ACT("activation", out=Lb[0:T, q * 512:q * 512 + w_], in_=ps[q % 2][0:T, 0:w_], func=AF.Exp, r=[ps[q % 2].b()], w=[Lb.b()])
        yield
        cbv = p7[0:T, 32:32 + 2 * T].rearrange("p (g i) -> p g i", i=T)
        cbmv = cbm[0:T, 0:2 * T].rearrange("p (g i) -> p g i", i=T)
        DVE("tensor_tensor", out=cbmv, in0=cbv, in1=triB.unsqueeze(1).to_broadcast([T, 2, T]), op=ALU.mult, r=[p7.b(), c2], w=[cbm.b()])
        scv = scT[0:T, 0:16 * T].rearrange("p (g r i) -> p g r i", g=2, i=T)
        Lv = Lb[0:T, 0:16 * T].rearrange("p (g r i) -> p g r i", g=2, i=T)
        DVE("tensor_tensor", out=scv, in0=Lv, in1=cbmv.unsqueeze(2).to_broadcast([T, 2, 8, T]), op=ALU.mult, r=[Lb.b(), cbm.b()], w=[scT.b()])
        ACT("activation", out=sm[0:T, 64:80], in_=p7[0:T, 16:32], func=AF.Exp, r=[p7.b()], w=[sm.b("wend")])
        decN = R1[:, 1024:1024 + 136]
        ACT("activation", out=decN, in_=p7[:, 320:456], func=AF.Exp, r=[p7.b(), R1.b()], w=[R1.b()])
        POOL("tensor_tensor", out=xw[0:T, :].rearrange("p (h d) -> p h d", d=64), in0=xdt[0:T, :].rearrange("p (h d) -> p h d", d=64),
             in1=sm[0:T, 64:80].unsqueeze(2).to_broadcast([T, 16, 64]), op=ALU.mult, r=[xdt.b(), sm.b("wend")], w=[xw.b()])
        yield
        for h in range(16):
            pb = ps[h // 8]
            PE("matmul", pb[0:T, (h % 8) * 64:(h % 8) * 64 + 64], lhsT=scT[0:T, h * T:(h + 1) * T], rhs=xdt[0:T, h * 64:(h + 1) * 64],
               start=True, stop=True, r=[scT.b(), xdt.b()], w=[pb.b()])
        for g in range(2):
            ACT("activation", out=ydsb[0:T, g * 512:(g + 1) * 512], in_=ps[g][0:T, :], func=AF.Copy, r=[ps[g].b()], w=[ydsb.b(g)])
        yield
        for c in range(8):
            pb = ps[c // 4]
            PE("transpose", pb[:, (c % 4) * 128:(c % 4) * 128 + T], ydsb[0:T, c * 128:(c + 1) * 128], ident(T),
               r=[ydsb.b(c // 4), cst.b()], w=[pb.b()])
        for c in range(8):
            pb = ps[2 + c // 4]
            PE("matmul", pb[:, (c % 4) * 128:(c % 4) * 128 + T], lhsT=ytok[0:T, c * 128:(c + 1) * 128], rhs=triB, start=True, stop=True,
               r=[ytok.b(c // 4), c2], w=[pb.b()])
        ecT = R1[:, 0:1024].rearrange("p (c t) -> p c t", t=128)[:, :, 0:T]
        for g in range(2):
            ACT("activation", out=R1[:, g * 512:(g + 1) * 512].rearrange("p (c t) -> p c t", t=128)[:, :, 0:T],
                in_=ps[2 + g][:, :].rearrange("p (c t) -> p c t", t=128)[:, :, 0:T], func=AF.Exp, r=[ps[2 + g].b(), R1.b()], w=[R1.b()])
        yield
        g1v = hTs[:, 0:8 * T].rearrange("p (c t) -> p c t", t=T)
        POOL("tensor_tensor", out=g1v, in0=xbcT[:, 0:8, cols], in1=pv[:, l, 56:64].unsqueeze(2).to_broadcast([128, 8, T]), op=ALU.mult,
             r=xsb[0:8] + [pv.b()], w=[hTs.b()])
        for g in range(2):
            DVE("tensor_tensor", out=g1v[:, g * 4:(g + 1) * 4, :], in0=g1v[:, g * 4:(g + 1) * 4, :],
                in1=ps[g][:, :].rearrange("p (c t) -> p c t", t=128)[:, :, 0:T], op=ALU.add, r=[hTs.b(), ps[g].b()], w=[hTs.b()])
        DVE("tensor_scalar", BtokM[0:T, :], Btok[0:T, :], cst2[0:T, 256:257], None, ALU.mult, r=[Btok.b(), c2], w=[BtokM.b()])
        for g in range(2):
            PE("matmul", ps[2 + g][:, :], lhsT=BtokM[0:T, g * 128:(g + 1) * 128], rhs=xw[0:T, g * 512:(g + 1) * 512], start=True, stop=True,
               r=[BtokM.b(), xw.b()], w=[ps[2 + g].b()])
        for g in range(2):
            ACT("activation", out=hT_p[:, l, g * 512:(g + 1) * 512], in_=ps[2 + g][:, :], func=AF.Copy, r=[ps[2 + g].b()], w=[hT_p.b(l)])
        yield
        hb2 = [hTbf[:, :], ydsb[:, 0:512].bitcast(BF16)]
        hb2b = [[hTbf.b()], [ydsb.b(0)]]
        for s_ in range(NSAMP):
            nt = nat[s_ % 2]
            hb, hbb = hb2[s_ % 2], hb2b[s_ % 2]
            sc = off + NMETA + DEC * s_
            DMA("sp", nt[:, :, :], st_ssd[l, s_].rearrange("(c p) n -> p c n", p=128), w=[nt.b()])
            for c in range(8):
                PE("transpose", ps[2 + c // 4][:, (c % 4) * 128:(c % 4) * 128 + 128], nt[:, c, :], ident(), r=[nt.b(), cst.b()], w=[ps[2 + c // 4].b()])
            for g in range(2):
                ACT("activation", out=hb[:, g * 512:(g + 1) * 512], in_=ps[2 + g][:, :], func=AF.Copy, r=[ps[2 + g].b()], w=hbb)
            for c in range(8):
                PE("matmul", ps[6][:, c * 64 + DEC * s_:c * 64 + DEC * s_ + DEC], lhsT=hb[:, c * 128:(c + 1) * 128],
                   rhs=xbcT[:, 10 + c // 4, sc:sc + DEC], start=True, stop=True, r=hbb + [xsb[10 + c // 4]], w=[ps[6].b()])
            DVE("tensor_scalar", BtokM[0:T, :], Btok[0:T, :], cst2[0:T, 257 + s_:258 + s_], None, ALU.mult, r=[Btok.b(), c2], w=[BtokM.b()])
            for c in range(8):
                PE("matmul", ps[c // 4][:, (c % 4) * 128:(c % 4) * 128 + 128], lhsT=xw[0:T, c * 128:(c + 1) * 128],
                   rhs=BtokM[0:T, (c // 4) * 128:(c // 4) * 128 + 128], start=True, stop=True, r=[xw.b(), BtokM.b()], w=[ps[c // 4].b()])
            dsv = R1[:, 1024:1024 + 136].rearrange("p (c b) -> p c b", b=17)[:, :, 1 + s_:2 + s_].to_broadcast([128, 8, 128])
            POOL("tensor_tensor", out=nt[:, :, :], in0=nt[:, :, :], in1=dsv, op=ALU.mult, r=[nt.b(), R1.b()], w=[nt.b()])
            for g in range(2):
                DVE("tensor_tensor", out=nt[:, g * 4:(g + 1) * 4, :], in0=nt[:, g * 4:(g + 1) * 4, :],
                    in1=ps[g][:, :].rearrange("p (c n) -> p c n", n=128), op=ALU.add, r=[nt.b(), ps[g].b()], w=[nt.b()])
            DMA("sp", o_ssd_s[l, s_].rearrange("(c p) n -> p c n", p=128), nt[:, :, :], r=[nt.b()])
            yield
        yo = ps[6][:, :].rearrange("p (c t) -> p c t", t=64)
        g1s = g1v[:, :, NMETA:T]
        tv = ytok[:, 0:512].rearrange("p (c t) -> p c t", t=64)
        DVE("tensor_tensor", out=tv, in0=yo, in1=ecT[:, :, NMETA:T], op=ALU.mult, r=[ps[6].b(), R1.b()], w=[ytok.b(0)])
        DVE("tensor_tensor", out=g1s, in0=g1s, in1=tv, op=ALU.add, r=[hTs.b(), ytok.b(0)], w=[hTs.b()])
        DVE("tensor_tensor", out=g1v, in0=g1v, in1=big[:, 16:24, cols], op=ALU.mult,
            r=[hTs.b()] + [big.b((16 + c, ti)) for c in range(8)], w=[hTs.b()])
        ACT("activation", out=sqg[:, 0:8 * T], in_=hTs[:, 0:8 * T], func=AF.Square, r=[hTs.b()], w=[sqg.b()])
        yield
        for c in range(8):
            PE("matmul", ps[2][:, 0:T], lhsT=onesb[:, :], rhs=sqg[:, c * T:(c + 1) * T], start=(c == 0), stop=(c == 7),
               r=[onesb.b(), sqg.b()], w=[ps[2].b()])
        ACT("activation", out=rsg[:, 0:T], in_=ps[2][:, 0:T], func=AF.Ln, bias=EPS, scale=1.0 / D, r=[ps[2].b()], w=[rsg.b()])
        ACT("activation", out=rrg[:, 0:T], in_=rsg[:, 0:T], func=AF.Exp, scale=-0.5, r=[rsg.b()], w=[rrg.b()])
        POOL("tensor_tensor", out=g1v, in0=g1v, in1=pv[:, l, 48:56].unsqueeze(2).to_broadcast([128, 8, T]), op=ALU.mult,
             r=[hTs.b(), pv.b()], w=[hTs.b()])
        DVE("tensor_tensor", out=big[:, 0:8, cols], in0=g1v, in1=rrg[:, 0:T].unsqueeze(1).to_broadcast([128, 8, T]), op=ALU.mult,
            r=[hTs.b(), rrg.b()], w=[big.b((c, ti)) for c in range(8)])
        ACT("activation", out=hTbf[:, :], in_=hT_p[:, l, :], func=AF.Copy, r=[hT_p.b(l)], w=[hTbf.b()])
        yield

    def interleave(*gens):
        gens = list(gens)
        while gens:
            for g in list(gens):
                try:
                    next(g)
                except StopIteration:
                    gens.remove(g)

    def mixer(l, gi, tiles):
        last_group = gi == len(GROUPS) - 1
        prenorm(tiles, pvc(l, 16, 24))
        hasS = tiles[0][0] == "S"
        if hasS:
            DMA("sp", sin_ssdc[:, :, :, :], st_ssdc[l], w=[sin_ssdc.b()])
            DMA("sp", sin_lruc[:, :, :, :], st_lruc[l], w=[sin_lruc.b()])
            DMA("sp", sin_lru[:, :, :], st_lru[l], w=[sin_lru.b()])
        for blk in range(20):
            wt = next_ws()
            for ti, (kind, off, n) in enumerate(tiles):
                pp = ps[cnt["bank"] % 4]
                cnt["bank"] += 1
                for kc in range(8):
                    PE("matmul", pp[:, 0:n], lhsT=wt[:, kc * 128:(kc + 1) * 128], rhs=xn[:, kc, off:off + n],
                                                               start=(kc == 0), stop=(kc == 7),
                       r=[wt.b(), xn.b((kc, ti))], w=[pp.b()])
                if blk < 8:
                    ACT("activation", out=big[:, 16 + blk, off:off + n], in_=pp[:, 0:n], func=AF.Silu,
                        r=[pp.b()], w=[big.b((16 + blk, ti))])
                else:
                    c = blk - 8
                    conv(kind, n, pp, pv[:, l, 140 + 4 * c:144 + 4 * c], pv[:, l, 128 + c:129 + c],
                         tail_xbc[:, l, c, :], tail_xbc.b((l, c)),
                         sin_ssdc[:, c, :, :], sin_ssdc.b(), sout_ssdc[:, c, :, :], sout_ssdc.b(),
                         lambda c=c, off=off, n=n: xbcT[:, c, off:off + n], [xbcT.b((c, ti))], True)
        def m2_gen():
          for ti, (kind, off, n) in enumerate(tiles):
            if kind == "S":
                yield from ssd_S(l, ti, off)
            else:
                if ti == 0:
                    ACT("activation", out=hTbf[:, :], in_=hT_p[:, l, :], func=AF.Copy, r=[hT_p.b(l)], w=[hTbf.b()])
                for c4 in range(n // 128):
                    yield from ssd_chunk(l, ti, off + 128 * c4, 128, True, (hT_p, hT_p[:, l, :]), hTbf)
          if hasS:
            DMA("sp", o_ssdc_s[l], sout_ssdc[:, :, :, :], r=[sout_ssdc.b()])
          if last_group:
            DMA("sp", o_ssd_p[l], hT_p[:, l, :], r=[hT_p.b(l)])
            DMA("sp", o_ssdc_p[l], tail_xbc[:, l, :, :], r=[tail_xbc.b((l, c)) for c in range(12)])

        def m3_gen():
          for c in range(8):
            wt = next_ws()
            wa = rot("wab", wab)
            DMA("pool", wa[:, :], w_ab[l, c], w=[wa.b()])
            for ti, (kind, off, n) in enumerate(tiles):
                pg, px = ps[4], ps[5]
                for kc in range(8):
                    PE("matmul", pg[:, 0:n], lhsT=wt[:, kc * 256:kc * 256 + 128], rhs=xn[:, kc, off:off + n],
                                                               start=(kc == 0), stop=(kc == 7),
                       r=[wt.b(), xn.b((kc, ti))], w=[pg.b()])
                for kc in range(8):
                    PE("matmul", px[:, 0:n], lhsT=wt[:, kc * 256 + 128:kc * 256 + 256], rhs=xn[:, kc, off:off + n],
                                                               start=(kc == 0), stop=(kc == 7),
                       r=[wt.b(), xn.b((kc, ti))], w=[px.b()])
                yield
                gl, g2 = lt["gl"], lt["g2"]
                ACT("activation", out=g2[:, 0:n], in_=pg[:, 0:n], func=AF.Square, r=[pg.b()], w=[g2.b()])
                DVE("tensor_scalar", g2[:, 0:n], g2[:, 0:n], 0.044715, 1.0, ALU.mult, ALU.add, r=[g2.b()], w=[g2.b()])
                DVE("tensor_tensor", out=g2[:, 0:n], in0=g2[:, 0:n], in1=pg[:, 0:n], op=ALU.mult, r=[g2.b(), pg.b()], w=[g2.b()])
                ACT("activation", out=g2[:, 0:n], in_=g2[:, 0:n], func=AF.Exp, scale=-1.5957691216057308, r=[g2.b()], w=[g2.b()])
                ACT("activation", out=g2[:, 0:n], in_=g2[:, 0:n], func=AF.Ln, bias=1.0, scale=1.0, r=[g2.b()], w=[g2.b()])
                ACT("activation", out=g2[:, 0:n], in_=g2[:, 0:n], func=AF.Exp, scale=-1.0, r=[g2.b()], w=[g2.b()])
                DVE("tensor_tensor", out=gl[:, 0:n], in0=g2[:, 0:n], in1=pg[:, 0:n], op=ALU.mult, r=[g2.b(), pg.b()], w=[gl.b()])
                yield
                ac = conv(kind, n, px, pv[:, l, 96 + 4 * c:100 + 4 * c], pv[:, l, 88 + c:89 + c],
                          tail_xr[:, l, c, :], tail_xr.b((l, c)),
                          sin_lruc[:, c, :, :], sin_lruc.b(), sout_lruc[:, c, :, :], sout_lruc.b(),
                          None, None, False)
                ACT("activation", out=xrbf[:, 0:n], in_=ac[:, 0:n], func=AF.Copy, r=[ac.b()], w=[xrbf.b()])
                yield
                pr, pi = ps[5], ps[4]
                PE("matmul", pr[:, 0:n], lhsT=wa[:, 0:128], rhs=xrbf[:, 0:n], start=True, stop=True,
                   r=[wa.b(), xrbf.b()], w=[pr.b()])
                PE("matmul", pi[:, 0:n], lhsT=wa[:, 128:256], rhs=xrbf[:, 0:n], start=True, stop=True,
                   r=[wa.b(), xrbf.b()], w=[pi.b()])
                r_, ig, a_, a2, mu, bt, hs = lt["r"], lt["ig"], lt["a"], lt["a2"], lt["mu"], lt["bt"], lt["hs"]
                ACT("activation", out=r_[:, 0:n], in_=pr[:, 0:n], func=AF.Exp, bias=pd[:, l, 64 + c:65 + c], scale=-1.0,
                    r=[pr.b(), pd.b()], w=[r_.b()])
                ACT("activation", out=r_[:, 0:n], in_=r_[:, 0:n], func=AF.Ln, bias=1.0, scale=1.0, r=[r_.b()], w=[r_.b()])
                ACT("activation", out=r_[:, 0:n], in_=r_[:, 0:n], func=AF.Exp, scale=-1.0, r=[r_.b()], w=[r_.b()])
                ACT("activation", out=ig[:, 0:n], in_=pi[:, 0:n], func=AF.Exp, bias=pd[:, l, 72 + c:73 + c], scale=-1.0,
                    r=[pi.b(), pd.b()], w=[ig.b()])
                ACT("activation", out=ig[:, 0:n], in_=ig[:, 0:n], func=AF.Ln, bias=1.0, scale=1.0, r=[ig.b()], w=[ig.b()])
                ACT("activation", out=ig[:, 0:n], in_=ig[:, 0:n], func=AF.Exp, scale=-1.0, r=[ig.b()], w=[ig.b()])
                ACT("activation", out=a_[:, 0:n], in_=r_[:, 0:n], func=AF.Exp, scale=pd[:, l, 16 + c:17 + c], r=[r_.b(), pd.b()], w=[a_.b()])
                ACT("activation", out=a2[:, 0:n], in_=r_[:, 0:n], func=AF.Exp, scale=pd[:, l, 24 + c:25 + c], r=[r_.b(), pd.b()], w=[a2.b()])
                ACT("activation", out=mu[:, 0:n], in_=a2[:, 0:n], func=AF.Ln, bias=1.0, scale=-1.0, r=[a2.b()], w=[mu.b()])
                ACT("activation", out=mu[:, 0:n], in_=mu[:, 0:n], func=AF.Exp, scale=0.5, r=[mu.b()], w=[mu.b()])
                DVE("tensor_tensor", out=bt[:, 0:n], in0=ig[:, 0:n], in1=mu[:, 0:n], op=ALU.mult, r=[ig.b(), mu.b()], w=[bt.b()])
                DVE("tensor_tensor", out=bt[:, 0:n], in0=bt[:, 0:n], in1=ac[:, 0:n], op=ALU.mult, r=[bt.b(), ac.b()], w=[bt.b()])
                if kind == "S":
                    a0 = a_[:, NMETA:n].rearrange("p (s t) -> p s t", t=DEC)[:, :, 0:1]
                    b0 = bt[:, NMETA:n].rearrange("p (s t) -> p s t", t=DEC)[:, :, 0:1]
                    DVE("tensor_tensor", out=tmp16[:, :].unsqueeze(2), in0=a0, in1=sin_lru[:, c, :].unsqueeze(2), op=ALU.mult,
                        r=[a_.b(), sin_lru.b()], w=[tmp16.b()])
                    DVE("tensor_tensor", out=b0, in0=b0, in1=tmp16[:, :].unsqueeze(2), op=ALU.add, r=[bt.b(), tmp16.b()], w=[bt.b()])
                    DVE("memset", a0, 0.0, r=[tmp16.b()], w=[a_.b()])
                    DVE("tensor_tensor_scan", out=hs[:, 0:n], data0=a_[:, 0:n], data1=bt[:, 0:n], initial=0.0, op0=ALU.mult, op1=ALU.add,
                        r=[a_.b(), bt.b()], w=[hs.b()])
                    DVE("tensor_copy", lru_carry[:, l, c:c + 1], hs[:, NMETA - 1:NMETA], r=[hs.b()], w=[lru_carry.b(l)])
                    DVE("tensor_copy", sout_lru[:, c, :].unsqueeze(2), hs[:, NMETA:n].rearrange("p (s t) -> p s t", t=DEC)[:, :, DEC - 1:DEC],
                         r=[hs.b()], w=[sout_lru.b()])
                else:
                    DVE("tensor_tensor_scan", out=hs[:, 0:n], data0=a_[:, 0:n], data1=bt[:, 0:n], initial=lru_carry[:, l, c:c + 1],
                                                       op0=ALU.mult, op1=ALU.add,
                        r=[a_.b(), bt.b(), lru_carry.b(l)], w=[hs.b()])
                    DVE("tensor_copy", lru_carry[:, l, c:c + 1], hs[:, n - 1:n], r=[hs.b()], w=[lru_carry.b(l)])
                DVE("tensor_tensor", out=big[:, 8 + c, off:off + n], in0=hs[:, 0:n], in1=gl[:, 0:n], op=ALU.mult,
                    r=[hs.b(), gl.b()], w=[big.b((8 + c, ti))])
        interleave(m2_gen(), m3_gen())
        if hasS:
            DMA("sp", o_lruc_s[l], sout_lruc[:, :, :, :], r=[sout_lruc.b()])
            DMA("sp", o_lru_s[l], sout_lru[:, :, :], r=[sout_lru.b()])
        if last_group:
            DMA("sp", o_lru_p[l], lru_carry[:, l, :], r=[lru_carry.b(l)])
            DMA("sp", o_lruc_p[l], tail_xr[:, l, :, :], r=[tail_xr.b((l, c)) for c in range(8)])
        down_postnorm(tiles, big, 16, lambda m: w_out[l, m], 2048, pvc(l, 24, 32))

    DMA("pool", wdt[:, :, :], w_dt.rearrange("l p n -> p l n"), w=[wdt.b()])
    xin_v = xT_in.rearrange("(kc p) n -> p kc n", p=128)
    yout_v = yT_out.rearrange("(kc p) n -> p kc n", p=128)
    for gi, (gbase, tiles) in enumerate(GROUPS[:int(_opt("KGROUPS", "99"))]):
        NT = sum(t[2] for t in tiles)
        CUR_TILES[0] = tiles
        for ti, (kind, off, n) in enumerate(tiles):
            for kc in range(8):
                DMA("sp", x[:, kc, off:off + n], xin_v[:, kc, gbase + off:gbase + off + n], w=[x.b((kc, ti))])
        for l in range(2):
            plan_ffn(l, 0)
            if stop_after == "ffn1":
                break
            plan_mixer(l)
            if stop_after == "mix":
                break
            plan_ffn(l, 1)
            if stop_after == "l0":
                break
        for l in range(2):
            PHASE[0] = f"g{gi}l{l}ffn1"
            ffn(l, 0, tiles)
            if stop_after == "ffn1":
                break
            PHASE[0] = f"g{gi}l{l}mix"
            mixer(l, gi, tiles)
            if stop_after == "mix":
                break
            PHASE[0] = f"g{gi}l{l}ffn2"
            ffn(l, 1, tiles)
            if stop_after == "l0":
                break
        for ti, (kind, off, n) in enumerate(tiles):
            for kc in range(8):
                DMA("sp", yout_v[:, kc, gbase + off:gbase + off + n], x[:, kc, off:off + n], r=[x.b((kc, ti))])

    emit(nc, stack, S)
    stack.close()
    return nc


def _consts():
    c = np.zeros((128, 512), np.float32)
    i = np.arange(128)
    c[:, 0:128] = np.eye(128, dtype=np.float32)
    c[:, 128:256] = (i[:, None] <= i[None, :]).astype(np.float32)
    c[:, 256:384] = (i[:, None] > i[None, :]).astype(np.float32)
    c[:, 384:512] = 1.0
    return c


def _consts2():
    c = np.zeros((128, 384), np.float32)
    bid = np.full(128, -1)
    bid[0:NMETA] = 0
    for s_ in range(NSAMP):
        bid[NMETA + DEC * s_:NMETA + DEC * (s_ + 1)] = 1 + s_
    i = np.arange(128)
    same = (bid[:, None] == bid[None, :]) & (bid[:, None] >= 0)
    c[:, 0:128] = ((i[:, None] <= i[None, :]) & same).astype(np.float32)
    c[:, 128:256] = ((i[:, None] > i[None, :]) & same).astype(np.float32)
    for b in range(17):
        c[:, 256 + b] = (bid == b).astype(np.float32)
    return c


def _col(v):
    return np.ascontiguousarray(v.reshape(-1, 128).T)


def prep_shared(inp):
    f = lambda a: np.asarray(a, dtype=np.float32)
    sh = {"cst": _consts(), "cst2": _consts2()}
    pv = np.zeros((2, 128, NPV), np.float32)
    for l in range(2):
        pv[l, :, 0:8] = _col(f(inp["ffn1_pre_g"])[l])
        pv[l, :, 8:16] = _col(f(inp["ffn1_post_g"])[l])
        pv[l, :, 16:24] = _col(f(inp["mix_pre_g"])[l])
        pv[l, :, 24:32] = _col(f(inp["mix_post_g"])[l])
        pv[l, :, 32:40] = _col(f(inp["ffn2_pre_g"])[l])
        pv[l, :, 40:48] = _col(f(inp["ffn2_post_g"])[l])
        pv[l, :, 48:56] = _col(f(inp["ssd_norm_g"])[l])
        pv[l, :, 56:64] = _col(np.repeat(f(inp["ssd_d"])[l], 64))
        pv[l, :, 64:72] = _col(f(inp["lru_ba"])[l])
        pv[l, :, 72:80] = _col(f(inp["lru_bx"])[l])
        pv[l, :, 80:88] = _col(f(inp["lru_lambda"])[l])
        pv[l, :, 88:96] = _col(f(inp["lru_conv_b"])[l])
        lw = f(inp["lru_conv_w"])[l]
        for k in range(4):
            pv[l, :, 96 + k:128:4] = _col(lw[k])
        pv[l, :, 128:140] = _col(f(inp["ssd_conv_b"])[l])
        sw = f(inp["ssd_conv_w"])[l]
        for k in range(4):
            pv[l, :, 140 + k:188:4] = _col(sw[k])
        pv[l, :, 188:204] = f(inp["ssd_dt_bias"])[l][None, :]
        pv[l, :, 204:220] = f(inp["ssd_a_log"])[l][None, :]
    sh["pv"] = pv
    w_gu = np.empty((2, 2, NJ, 128, 8, 256), np.float32)
    w_d = np.empty((2, 2, 8, 128, NJ, 128), np.float32)
    for fi, nm in enumerate(["ffn1", "ffn2"]):
        wg = f(inp[nm + "_wg"]).reshape(2, 8, 128, NJ, 128)
        wu = f(inp[nm + "_wu"]).reshape(2, 8, 128, NJ, 128)
        w_gu[:, fi, :, :, :, 0:128] = wg.transpose(0, 3, 2, 1, 4)
        w_gu[:, fi, :, :, :, 128:256] = wu.transpose(0, 3, 2, 1, 4)
        wd = f(inp[nm + "_wd"]).reshape(2, NJ, 128, 8, 128)
        w_d[:, fi] = wd.transpose(0, 3, 2, 1, 4)
    sh["w_gu"] = w_gu.reshape(2, 2, NJ, 128, 2048)
    sh["w_d"] = w_d.reshape(2, 2, 8, 128, DFF)
    win = f(inp["w_in"]).reshape(2, 8, 128, 4624)
    zx = win[..., 0:2560].reshape(2, 8, 128, 20, 128)
    sh["w_zx"] = np.ascontiguousarray(zx.transpose(0, 3, 2, 1, 4)).reshape(2, 20, 128, 1024)
    sh["w_dt"] = np.ascontiguousarray(win[..., 2560:2576].transpose(0, 2, 1, 3)).reshape(2, 128, 128)
    gate = win[..., 2576:3600].reshape(2, 8, 128, 8, 128)
    xr = win[..., 3600:4624].reshape(2, 8, 128, 8, 128)
    wl = np.empty((2, 8, 128, 8, 256), np.float32)
    wl[..., 0:128] = gate.transpose(0, 3, 2, 1, 4)
    wl[..., 128:256] = xr.transpose(0, 3, 2, 1, 4)
    sh["w_lru"] = wl.reshape(2, 8, 128, 2048)
    wab = np.zeros((2, 8, 128, 256), np.float32)
    wa, wx = f(inp["lru_wa"]), f(inp["lru_wx"])
    for c in range(8):
        for h in range(2):
            wab[:, c, h * 64:(h + 1) * 64, h * 64:(h + 1) * 64] = wa[:, 2 * c + h]
            wab[:, c, h * 64:(h + 1) * 64, 128 + h * 64:128 + (h + 1) * 64] = wx[:, 2 * c + h]
    sh["w_ab"] = wab
    wo = f(inp["w_out"]).reshape(2, 16, 128, 8, 128)
    sh["w_out"] = np.ascontiguousarray(wo.transpose(0, 3, 2, 1, 4)).reshape(2, 8, 128, 2048)
    return sh


def prep_core(inp, b, sh):
    f = lambda a: np.asarray(a, dtype=np.float32)
    m = dict(sh)
    s0, s1 = NSAMP * b, NSAMP * (b + 1)
    rows = np.concatenate([f(inp["meta_tokens"]), f(inp["x_sample"])[s0:s1].reshape(NSAMP * DEC, D), f(inp["x_prompt"])[b]], axis=0)
    m["xT_in"] = np.ascontiguousarray(rows.T)
    m["st_ssd"] = np.ascontiguousarray(f(inp["state_ssd"])[:, s0:s1].reshape(2, NSAMP, D, 128))
    sc = f(inp["state_ssd_conv"])[:, s0:s1]
    m["st_ssdc"] = np.ascontiguousarray(sc.reshape(2, NSAMP, 3, 12, 128).transpose(0, 4, 3, 1, 2))
    m["st_lru"] = np.ascontiguousarray(f(inp["state_lru"])[:, s0:s1].reshape(2, NSAMP, 8, 128).transpose(0, 3, 2, 1))
    lc = f(inp["state_lru_conv"])[:, s0:s1]
    m["st_lruc"] = np.ascontiguousarray(lc.reshape(2, NSAMP, 3, 8, 128).transpose(0, 4, 3, 1, 2))
    return m


_NC_CACHE = {}


def kernel(**inputs):
    n = 8
    sh = prep_shared(inputs)
    in_maps = [prep_core(inputs, b, sh) for b in range(n)]
    if "nc" not in _NC_CACHE:
        _NC_CACHE["nc"] = build_program()
    nc = _NC_CACHE["nc"]
    res = run_bass_kernel_spmd(nc, in_maps, core_ids=list(range(n)))
    return assemble(res.results, n)


def assemble(results, n):
    B = n
    y_prompt = np.empty((B, SEQ, D), np.float32)
    y_sample = np.empty((B * NSAMP, DEC, D), np.float32)
    ssd_p = np.empty((2, B, 16, 64, 128), np.float32)
    ssdc_p = np.empty((2, B, 3, 1536), np.float32)
    lru_p = np.empty((2, B, D), np.float32)
    lruc_p = np.empty((2, B, 3, D), np.float32)
    ssd_s = np.empty((2, B * NSAMP, 16, 64, 128), np.float32)
    ssdc_s = np.empty((2, B * NSAMP, 3, 1536), np.float32)
    lru_s = np.empty((2, B * NSAMP, D), np.float32)
    lruc_s = np.empty((2, B * NSAMP, 3, D), np.float32)
    for b in range(B):
        r = results[b]
        yT = r["yT_out"]
        y_prompt[b] = yT[:, STILE:].T
        y_sample[b * NSAMP:(b + 1) * NSAMP] = yT[:, NMETA:STILE].T.reshape(NSAMP, DEC, D)
        ssd_p[:, b] = r["o_ssd_p"].transpose(0, 2, 1).reshape(2, 16, 64, 128)
        ssdc_p[:, b] = r["o_ssdc_p"].transpose(0, 3, 2, 1).reshape(2, 3, 1536)
        lru_p[:, b] = r["o_lru_p"].transpose(0, 2, 1).reshape(2, D)
        lruc_p[:, b] = r["o_lruc_p"].transpose(0, 3, 2, 1).reshape(2, 3, D)
        sl = slice(b * NSAMP, (b + 1) * NSAMP)
        ssd_s[:, sl] = r["o_ssd_s"].reshape(2, NSAMP, 16, 64, 128)
        ssdc_s[:, sl] = r["o_ssdc_s"].transpose(0, 3, 4, 2, 1).reshape(2, NSAMP, 3, 1536)
        lru_s[:, sl] = r["o_lru_s"].transpose(0, 3, 2, 1).reshape(2, NSAMP, D)
        lruc_s[:, sl] = r["o_lruc_s"].transpose(0, 3, 4, 2, 1).reshape(2, NSAMP, 3, D)
    return (y_prompt, y_sample, ssd_p, ssdc_p, lru_p, lruc_p, ssd_s, ssdc_s, lru_s, lruc_s)
```

```python
import os
from contextlib import ExitStack
import numpy as np
import concourse.bass as bass
import concourse.mybir as mybir
from concourse.bass_utils import run_bass_kernel_spmd

F32 = mybir.dt.float32
BF16 = mybir.dt.bfloat16


def _opt(name, default=None):
    return default

AF = mybir.ActivationFunctionType
ALU = mybir.AluOpType

D = 1024
DFF = 2816
NJ = DFF // 128
SEQ = 2048
NMETA = 16
NSAMP = 16
DEC = 4
EPS = 1e-6
NTOK = NMETA + NSAMP * DEC + SEQ
STILE = NMETA + NSAMP * DEC
TSPLIT = int(_opt("KTSPLIT", "1"))
if TSPLIT == 2:
    GROUPS = [
        (0, [("S", 0, STILE), ("T", STILE, 256), ("T", STILE + 256, 256)]),
        (STILE + 512, [("T", 0, 256), ("T", 256, 256)]),
        (STILE + 1024, [("T", 0, 256), ("T", 256, 256)]),
        (STILE + 1536, [("T", 0, 256), ("T", 256, 256)]),
    ]
else:
    GROUPS = [
        (0, [("S", 0, STILE), ("T", STILE, 512)]),
        (STILE + 512, [("T", 0, 512)]),
        (STILE + 1024, [("T", 0, 512)]),
        (STILE + 1536, [("T", 0, 512)]),
    ]
NTMAX = STILE + 512
NPV = 220
ENGS = ["pe", "act", "dve", "pool", "sp"]
SEG_EDGES = [0, STILE, 256, STILE + 256, 512, NTMAX]
CUR_TILES = [None]
PHASE = ["setup"]


class Buf:
    __slots__ = ("name", "w", "r")

    def __init__(self, name):
        self.name = name
        self.w = None
        self.r = {}


class Op:
    __slots__ = ("eng", "fn", "deps", "pos", "signal", "sigval", "dma", "dsem", "dval", "gid", "tag", "t0", "t1")


class TT:
    def __init__(self, name, handle, psum=False, tok=False):
        self.name = name
        self.h = handle
        self.bufs = {}
        self.psum = psum
        self.tok = tok

    def b(self, key=0):
        if self.psum:
            key = 0
        if self.tok:
            c, ti = key
            if ti == "all":
                segs = range(len(SEG_EDGES) - 1)
            else:
                if isinstance(ti, tuple):
                    lo_, hi_ = ti
                else:
                    kind, off, n = CUR_TILES[0][ti]
                    lo_, hi_ = off, off + n
                segs = [i for i in range(len(SEG_EDGES) - 1) if SEG_EDGES[i] < hi_ and SEG_EDGES[i + 1] > lo_]
            return [self._b((c, sg_)) for sg_ in segs]
        return self._b(key)

    def _b(self, key):
        v = self.bufs.get(key)
        if v is None:
            v = self.bufs[key] = Buf(f"{self.name}:{key}")
        return v

    def __getitem__(self, idx):
        return self.h[idx]


class View:
    def __init__(self, ap, bufs):
        self.ap = ap
        self.bufs = bufs

    def b(self, key=0):
        return self.bufs

    def __getitem__(self, idx):
        return self.ap[idx]


class Sched:
    def __init__(self):
        self.ops = {e: [] for e in ENGS}
        self.n = 0

    def add(self, eng, fn, reads=(), writes=(), dma=False):
        op = Op()
        op.eng, op.fn, op.dma = eng, fn, dma
        op.signal, op.sigval, op.dsem, op.dval = False, 0, None, 0
        op.gid = self.n
        op.tag = PHASE[0]
        self.n += 1
        def _flat(lst):
            out_ = []
            for b_ in lst:
                if isinstance(b_, (list, tuple)):
                    out_.extend(_flat(b_))
                else:
                    out_.append(b_)
            return out_
        reads, writes = _flat(reads), _flat(writes)
        deps = {}
        ex = [b for b in reads if b.name.startswith("ps")]
        if ex:
            reads = [b for b in reads if not b.name.startswith("ps")]
            writes = list(writes) + ex
        for b in reads:
            if b.w is not None:
                deps[b.w.gid] = b.w
        for b in writes:
            if b.w is not None:
                deps[b.w.gid] = b.w
            for k, r in b.r.items():
                if isinstance(r, list):
                    for rr in r:
                        deps[rr.gid] = rr
                else:
                    deps[r.gid] = r
        op.deps = list(deps.values())
        for b in writes:
            b.w = op
            b.r = {}
        for b in reads:
            b.r.setdefault("all", []).append(op)
        op.pos = len(self.ops[eng])
        self.ops[eng].append(op)
        return op


S = None


def PE(name, *a, r=(), w=(), **k):
    return S.add("pe", (name, a, k), r, w)


def ACT(name, *a, r=(), w=(), **k):
    return S.add("act", (name, a, k), r, w)


def DVE(name, *a, r=(), w=(), **k):
    return S.add("dve", (name, a, k), r, w)


def POOL(name, *a, r=(), w=(), **k):
    return S.add("pool", (name, a, k), r, w)


def DMA(q, out, in_, r=(), w=()):
    return S.add(q, ("dma_start", (), dict(out=out, in_=in_)), r, w, dma=True)


PE_FIX = float(_opt("KPEFIX", "0.05"))
PE_RATE = float(_opt("KPERATE", "2000"))
ENG_SCALE = {"act": 1.0, "dve": 1.0, "pool": 1.0, "dma": 1.0}


def _op_cost(op):
    name, a, k = op.fn
    out = k.get("out", a[0] if a else None)
    try:
        shp = list(out.shape)
    except Exception:
        shp = [128, 64]
    free = 1
    for d in shp[1:]:
        free *= int(d)
    if op.dma:
        return 0.15 if op.eng == "sp" else 0.65, (2.2 + free * shp[0] * 4 / 200e3) * ENG_SCALE["dma"]
    if op.eng == "pe":
        f32 = False
        try:
            f32 = (k.get("lhsT", a[1] if len(a) > 1 else None).dtype == F32) and name == "matmul"
        except Exception:
            pass
        t = PE_FIX + max(free, 64) / PE_RATE * (4 if f32 else 1)
        return t, t + 0.15
    if op.eng == "act":
        t = (0.2 + free / 1100.0) * ENG_SCALE["act"]
        return t, t + 0.05
    if op.eng == "dve":
        t = (0.1 + free / 900.0 * (2 if name == "tensor_tensor_scan" else 1)) * ENG_SCALE["dve"]
        return t, t + 0.05
    t = (0.15 + free / 450.0) * ENG_SCALE["pool"]
    return t, t + 0.05


def list_schedule(sched):
    import heapq
    allops = []
    for e in ENGS:
        allops.extend(sched.ops[e])
    succ = {op.gid: [] for op in allops}
    indeg = {}
    for op in allops:
        indeg[op.gid] = len(op.deps)
        for d in op.deps:
            succ[d.gid].append(op)
    ready_t = {op.gid: 0.0 for op in allops}
    rank = {}
    for op in sorted(allops, key=lambda o: -o.gid):
        lat_ = _op_cost(op)[1]
        rank[op.gid] = lat_ + max([rank[s_.gid] for s_ in succ[op.gid]], default=0.0)
    use_rank = _opt("KRANK", "0") == "1"
    fin = {}
    heaps = {e: [] for e in ENGS}
    for op in allops:
        if indeg[op.gid] == 0:
            heapq.heappush(heaps[op.eng], (0.0, op.gid, op))
    t_eng = {e: 0.0 for e in ENGS}
    new = {e: [] for e in ENGS}
    remaining = len(allops)
    while remaining:
        best = None
        for e in ENGS:
            h = heaps[e]
            if not h:
                continue
            rt, gid, op = h[0]
            st = max(rt, t_eng[e])
            key = (st, gid)
            if best is None or key < best[0]:
                best = (key, e)
        (st, _), e = best
        h = heaps[e]
        cands = []
        while h and h[0][0] <= st + 1e-9:
            cands.append(heapq.heappop(h))
        if use_rank:
            cands.sort(key=lambda c: (-rank[c[1]], c[1]))
        else:
            cands.sort(key=lambda c: c[1])
        rt, gid, op = cands[0]
        for c in cands[1:]:
            heapq.heappush(h, c)
        occ, lat = _op_cost(op)
        t_eng[e] = st + occ
        fin[gid] = st + lat
        op.t0, op.t1 = st, st + occ
        new[e].append(op)
        remaining -= 1
        for sop in succ[gid]:
            if fin[gid] > ready_t[sop.gid]:
                ready_t[sop.gid] = fin[gid]
            indeg[sop.gid] -= 1
            if indeg[sop.gid] == 0:
                heapq.heappush(heaps[sop.eng], (ready_t[sop.gid], sop.gid, sop))
    for e in ENGS:
        sched.ops[e] = new[e]
        for i, op in enumerate(new[e]):
            op.pos = i
    return max(t_eng.values())


def emit(nc, stack, sched):
    if _opt("KNOSCHED", "0") != "1":
        est = list_schedule(sched)
        if _opt("KVERBOSE"):
            print("list-schedule estimate (us):", est)
    engobj = {"pe": "tensor", "act": "scalar", "dve": "vector", "pool": "gpsimd", "sp": "sync"}
    esem = {e: stack.enter_context(nc.semaphore(f"s_{e}")) for e in ENGS}
    NDS = {"sp": 24, "pool": 24, "act": 4, "pe": 1, "dve": 1}
    dsems = {e: [stack.enter_context(nc.semaphore(f"d_{e}{i}")) for i in range(NDS[e])] for e in ENGS}
    for e in ENGS:
        for op in sched.ops[e]:
            for d in op.deps:
                if d.dma:
                    continue
                if d.eng != op.eng:
                    d.signal = True
                elif op.dma:
                    d.signal = True
                elif d.eng != "pe":
                    d.signal = True
    dcount = {e: 0 for e in ENGS}
    dprev = {}
    final_d = {}
    for e in ENGS:
        c = 0
        for op in sched.ops[e]:
            if op.dma:
                k = dcount[e]
                dcount[e] += 1
                s = dsems[e][k % NDS[e]]
                op.dsem = s
                op.dval = 16 * (k // NDS[e] + 1)
                final_d[(e, k % NDS[e])] = (s, op.dval)
            elif op.signal:
                c += 1
                op.sigval = c

    def run(e, eng):
        waited = {}

        def wait(sem, val):
            key = id(sem)
            if waited.get(key, 0) < val:
                eng.wait_ge(sem, val)
                waited[key] = val

        for op in sched.ops[e]:
            for d in op.deps:
                if d.dma:
                    wait(d.dsem, d.dval)
                elif d.eng != op.eng or op.dma or d.eng != "pe":
                    wait(esem[d.eng], d.sigval)
            if op.dma and op.dval > 16:
                wait(op.dsem, op.dval - 16)
            nm_, a_, k_ = op.fn
            ins = getattr(eng, nm_)(*a_, **k_)
            if op.dma:
                ins.then_inc(op.dsem, 16)
            elif op.signal:
                ins.then_inc(esem[e], 1)
        if e == "sp":
            for (s, v) in final_d.values():
                wait(s, v)

    with nc.Block() as block:
        @block.tensor
        def _(t):
            run("pe", t)

        @block.scalar
        def _(t):
            run("act", t)

        @block.vector
        def _(t):
            run("dve", t)

        @block.gpsimd
        def _(t):
            run("pool", t)

        @block.sync
        def _(t):
            run("sp", t)


def build_program(stop_after=None):
    global S
    S = Sched()
    nc = bass.Bass("TRN2", target_bir_lowering=False)
    stack = ExitStack()

    def din(name, shape):
        return nc.dram_tensor(name, list(shape), F32, kind="ExternalInput").ap()

    def dout(name, shape):
        return nc.dram_tensor(name, list(shape), F32, kind="ExternalOutput").ap()

    xT_in = din("xT_in", [D, NTOK])
    st_ssd = din("st_ssd", [2, NSAMP, D, 128])
    st_ssdc = din("st_ssdc", [2, 128, 12, NSAMP, 3])
    st_lru = din("st_lru", [2, 128, 8, NSAMP])
    st_lruc = din("st_lruc", [2, 128, 8, NSAMP, 3])
    cst_in = din("cst", [128, 512])
    cst2_in = din("cst2", [128, 384])
    pv_in = din("pv", [2, 128, NPV])
    w_gu = din("w_gu", [2, 2, NJ, 128, 2048])
    w_d = din("w_d", [2, 2, 8, 128, DFF])
    w_zx = din("w_zx", [2, 20, 128, 1024])
    w_dt = din("w_dt", [2, 128, 128])
    w_lru = din("w_lru", [2, 8, 128, 2048])
    w_ab = din("w_ab", [2, 8, 128, 256])
    w_out = din("w_out", [2, 8, 128, 2048])

    yT_out = dout("yT_out", [D, NTOK])
    o_ssd_p = dout("o_ssd_p", [2, 128, D])
    o_ssdc_p = dout("o_ssdc_p", [2, 128, 12, 3])
    o_lru_p = dout("o_lru_p", [2, 128, 8])
    o_lruc_p = dout("o_lruc_p", [2, 128, 8, 3])
    o_ssd_s = dout("o_ssd_s", [2, NSAMP, D, 128])
    o_ssdc_s = dout("o_ssdc_s", [2, 128, 12, NSAMP, 3])
    o_lru_s = dout("o_lru_s", [2, 128, 8, NSAMP])
    o_lruc_s = dout("o_lruc_s", [2, 128, 8, NSAMP, 3])

    def sb(name, shape, dt=F32, tok=False):
        return TT(name, stack.enter_context(nc.sbuf_tensor("sb_" + name, list(shape), dt)), tok=tok)

    x = sb("x", [128, 8, NTMAX], tok=True)
    xn = sb("xn", [128, 8, NTMAX], BF16, tok=True)
    big = sb("big", [128, 24, NTMAX], BF16, tok=True)
    o = sb("o", [128, 8, NTMAX], tok=True)
    xbcT = sb("xbcT", [128, 12, NTMAX], BF16, tok=True)
    cst = sb("cst", [128, 512])
    cst2 = sb("cst2", [128, 384])
    identb = sb("identb", [128, 128], BF16)
    onesb = sb("onesb", [128, 128], BF16)
    pv = sb("pvt", [128, 2, NPV])
    pd = sb("pdt", [128, 2, 96])
    wslots = [sb(f"ws{i}", [128, DFF], BF16) for i in range(3)]
    wdt = sb("wdt", [128, 2, 128], BF16)
    wab = [sb(f"wab{i}", [128, 256], BF16) for i in range(2)]
    hT_p = sb("hT_p", [128, 2, D])
    tail_xbc = sb("tail_xbc", [128, 2, 12, 3])
    tail_xr = sb("tail_xr", [128, 2, 8, 3])
    lru_carry = sb("lru_carry", [128, 2, 8])
    sin_ssdc = sb("sin_ssdc", [128, 12, NSAMP, 3])
    sin_lruc = sb("sin_lruc", [128, 8, NSAMP, 3])
    sin_lru = sb("sin_lru", [128, 8, NSAMP])
    sout_ssdc, sout_lruc = sin_ssdc, sin_lruc
    sout_lru = sb("sout_lru", [128, 8, NSAMP])
    rs = [sb(f"rs{i}", [128, 512]) for i in range(1)]
    rr = [sb(f"rr{i}", [128, 512]) for i in range(2)]
    sg = [sb(f"sg{i}", [128, 512]) for i in range(2)]
    tmpn = sg
    xp = [sb(f"xp{i}", [128, 515]) for i in range(2)]
    xps = [sb(f"xps{i}", [128, NSAMP, 7]) for i in range(2)]
    acc = [sb(f"acc{i}", [128, 512]) for i in range(2)]
    lt = {k: sb(f"lt_{k}", [128, 512]) for k in ["r", "ig", "a", "a2", "hs", "g2"]}
    lt["mu"], lt["bt"], lt["gl"] = lt["a2"], lt["ig"], lt["g2"]
    xrbf = sb("xrbf", [128, 512], BF16)
    tmp16 = sb("tmp16", [128, NSAMP])
    R1 = sb("R1", [128, 2048])
    Lb = sb("Lb", [128, 2048], BF16)
    scT = sb("scT", [128, 2048], BF16)
    cbm = sb("cbm", [128, 256])
    xdt = sb("xdt", [128, D], BF16)
    xw = sb("xw", [128, D], BF16)
    Btok = sb("Btok", [128, 256], BF16)
    ydsb = sb("ydsb", [128, D])
    ytok = sb("ytok", [128, D])
    sm = sb("sm", [128, 128])
    hTs = sb("hTs", [128, D])
    hTbf = sb("hTbf", [128, D], BF16)
    nat = [sb(f"nat{i}", [128, 8, 128]) for i in range(2)]
    BtokM = sb("BtokM", [128, 256], BF16)
    nato = nat
    sqg = sb("sqg", [128, D], BF16)
    rsg = sb("rsg", [128, 128])
    rrg = sb("rrg", [128, 128])

    oflat = o.h[:, :, :].rearrange("p k n -> p (k n)")
    oall = [o.b((k, "all")) for k in range(8)]
    g1o = View(oflat[:, 0:1024], oall[0:2])
    ygo = View(oflat[:, 2 * NTMAX:2 * NTMAX + 1024], oall[2:4])
    ps = [TT(f"ps{i}", stack.enter_context(nc.psum_tensor(f"ps{i}", [128, 512], F32)), psum=True) for i in range(8)]

    ident = lambda T=128: cst[0:T, 0:T]
    tri = lambda T=128: cst[0:T, 128:128 + T]
    Um = lambda T=128: cst[0:T, 256:256 + T]
    onesf = lambda T=128: cst[0:T, 384:512]

    def pvc(l, lo, hi=None):
        return pv[:, l, lo:(hi if hi is not None else lo + 1)]

    DMA("sp", cst[:, :], cst_in[:, :], w=[cst.b()])
    DMA("sp", cst2[:, :], cst2_in[:, :], w=[cst2.b()])
    DMA("sp", pv[:, :, :], pv_in.rearrange("l p n -> p l n"), w=[pv.b()])
    ACT("activation", out=identb[:, :], in_=cst[:, 0:128], func=AF.Copy, r=[cst.b()], w=[identb.b()])
    ACT("activation", out=onesb[:, :], in_=cst[:, 384:512], func=AF.Copy, r=[cst.b()], w=[onesb.b()])
    for l in range(2):
        ACT("activation", out=pd[:, l, 56:64], in_=pv[:, l, 80:88], func=AF.Exp, scale=-1.0, r=[pv.b()], w=[pd.b()])
        ACT("activation", out=pd[:, l, 56:64], in_=pd[:, l, 56:64], func=AF.Ln, bias=1.0, scale=1.0, r=[pd.b()], w=[pd.b()])
        ACT("activation", out=pd[:, l, 0:16], in_=pv[:, l, 204:220], func=AF.Exp, r=[pv.b(), pd.b()], w=[pd.b()])
        DVE("tensor_scalar", pd[:, l, 16:24], pd[:, l, 56:64], -8.0, None, ALU.mult, r=[pd.b()], w=[pd.b()])
        DVE("tensor_scalar", pd[:, l, 24:32], pd[:, l, 56:64], -16.0, None, ALU.mult, r=[pd.b()], w=[pd.b()])
        DVE("tensor_scalar", pd[:, l, 0:16], pd[:, l, 0:16], -1.0, None, ALU.mult, r=[pd.b()], w=[pd.b()])
        DVE("tensor_scalar", pd[:, l, 32:40], pv[:, l, 8:16], 0.5, None, ALU.mult, r=[pv.b(), pd.b()], w=[pd.b()])
        DVE("tensor_scalar", pd[:, l, 40:48], pv[:, l, 40:48], 0.5, None, ALU.mult, r=[pv.b(), pd.b()], w=[pd.b()])
        DVE("tensor_scalar", pd[:, l, 64:80], pv[:, l, 64:80], -1.0, None, ALU.mult, r=[pv.b(), pd.b()], w=[pd.b()])
        POOL("memset", lru_carry[:, l, :], 0.0, w=[lru_carry.b(l)])

    cnt = {"ws": 0, "bank": 0, "rs": 0, "sg": 0, "xp": 0, "wab": 0, "nat": 0, "bankd": 0, "nb": 0}

    wplan = []
    wstate = {"issued": 0, "used": 0}

    def w_issue(i):
        ncols, src, split = wplan[i]
        wt = wslots[i % 3]
        if split:
            DMA("pool", wt[:, 0:ncols].rearrange("p (a b) -> p a b", a=2), src.rearrange("p (a b) -> p a b", a=2), w=[wt.b()])
        else:
            DMA("pool", wt[:, 0:ncols], src, w=[wt.b()])

    def next_ws():
        u = wstate["used"]
        while wstate["issued"] < min(len(wplan), u + 3):
            w_issue(wstate["issued"])
            wstate["issued"] += 1
        wstate["used"] = u + 1
        return wslots[u % 3]

    def plan_ffn(l, f):
        for j in range(NJ):
            wplan.append((2048, w_gu[l, f, j], False))
        for m in range(8):
            wplan.append((DFF, w_d[l, f, m], True))

    def plan_mixer(l):
        for blk in range(20):
            wplan.append((1024, w_zx[l, blk], False))
        for c in range(8):
            wplan.append((2048, w_lru[l, c], False))
        for m in range(8):
            wplan.append((2048, w_out[l, m], False))

    def rot(key, lst):
        i = cnt[key] % len(lst)
        cnt[key] += 1
        return lst[i]

    def rms_scale(pbank, n):
        i = cnt["rs"] % 4
        cnt["rs"] += 1
        a, b = rs[0], rr[i // 2]
        co = (i % 2) * 256
        ACT("activation", out=a[:, co:co + n], in_=pbank[:, 0:n], func=AF.Ln, bias=EPS, scale=1.0 / D, r=[pbank.b()], w=[a.b(i % 2)])
        ACT("activation", out=b[:, co:co + n], in_=a[:, co:co + n], func=AF.Exp, scale=-0.5, r=[a.b(i % 2)], w=[b.b(i % 2)])
        return b, co, i % 2

    def subtiles(off, n):
        if n <= 256:
            return [(off, n)]
        h = n // 2
        return [(off, h), (off + h, n - h)]

    def prenorm(tiles, gcol):
        for ti, (kind, off0, n0) in enumerate(tiles):
          for (off, n) in subtiles(off0, n0):
            rg = (off, off + n)
            pbk = ps[6 + (cnt["nb"] % 2)]
            cnt["nb"] += 1
            for kc in range(8):
                ACT("activation", out=xbcT[:, kc, off:off + n], in_=x[:, kc, off:off + n], func=AF.Square,
                    r=[x.b((kc, rg))], w=[xbcT.b((kc, rg))])
            for kc in range(8):
                PE("matmul", pbk[:, 0:n], lhsT=onesb[:, :], rhs=xbcT[:, kc, off:off + n], start=(kc == 0), stop=(kc == 7),
                   r=[onesb.b(), xbcT.b((kc, rg))], w=[pbk.b()])
            rt, co, rk = rms_scale(pbk, n)
            for kc in range(8):
                DVE("scalar_tensor_tensor", out=xn[:, kc, off:off + n], in0=x[:, kc, off:off + n],
                                                                    scalar=gcol[:, kc:kc + 1], in1=rt[:, co:co + n],
                                                                    op0=ALU.mult, op1=ALU.mult,
                    r=[x.b((kc, rg)), rt.b(rk), pv.b(), pd.b()], w=[xn.b((kc, rg))])

    def down_postnorm(tiles, src, KC, wsrc_fn, wrow, gcol):
        for m in range(8):
            wt = next_ws()
            for ti, (kind, off, n) in enumerate(tiles):
                pb = ps[4 + (cnt["bankd"] % 2)]
                cnt["bankd"] += 1
                for k in range(KC):
                    PE("matmul", pb[:, 0:n], lhsT=wt[:, k * 128:(k + 1) * 128], rhs=src[:, k, off:off + n],
                                                             start=(k == 0), stop=(k == KC - 1),
                       r=[wt.b(), big.b((k, ti))], w=[pb.b()])
                DVE("tensor_copy", o[:, m, off:off + n], pb[:, 0:n],
                    r=[pb.b()], w=[o.b((m, ti))])
                ACT("activation", out=xn[:, m, off:off + n], in_=pb[:, 0:n], func=AF.Square,
                    r=[pb.b()], w=[xn.b((m, ti))])
        for ti, (kind, off0, n0) in enumerate(tiles):
          for (off, n) in subtiles(off0, n0):
            rg = (off, off + n)
            pbk = ps[6 + (cnt["nb"] % 2)]
            cnt["nb"] += 1
            for m in range(8):
                PE("matmul", pbk[:, 0:n], lhsT=onesb[:, :], rhs=xn[:, m, off:off + n], start=(m == 0), stop=(m == 7),
                   r=[onesb.b(), xn.b((m, rg))], w=[pbk.b()])
            rt, co, rk = rms_scale(pbk, n)
            for m in range(8):
                tm = rot("sg", tmpn)
                DVE("scalar_tensor_tensor", out=tm[:, 0:n], in0=o[:, m, off:off + n],
                                                                        scalar=gcol[:, m:m + 1], in1=rt[:, co:co + n],
                                                                        op0=ALU.mult, op1=ALU.mult,
                    r=[o.b((m, rg)), rt.b(rk), pv.b(), pd.b()], w=[tm.b()])
                eng_add = POOL if (m % 2 == 0) else DVE
                eng_add("tensor_tensor", out=x[:, m, off:off + n], in0=x[:, m, off:off + n], in1=tm[:, 0:n], op=ALU.add,
                        r=[x.b((m, rg)), tm.b()], w=[x.b((m, rg))])

    def ffn(l, f, tiles):
        gpre = pvc(l, 0, 8) if f == 0 else pvc(l, 32, 40)
        gpost = pd[:, l, 32:40] if f == 0 else pd[:, l, 40:48]
        prenorm(tiles, gpre)
        for j in range(NJ):
            wt = next_ws()
            for ti, (kind, off, n) in enumerate(tiles):
                bi = (cnt["bank"] % 2) * 2
                cnt["bank"] += 1
                pg, pu = ps[bi], ps[bi + 1]
                for kc in range(8):
                    PE("matmul", pg[:, 0:n], lhsT=wt[:, kc * 256:kc * 256 + 128], rhs=xn[:, kc, off:off + n],
                                                               start=(kc == 0), stop=(kc == 7),
                       r=[wt.b(), xn.b((kc, ti))], w=[pg.b()])
                for kc in range(8):
                    PE("matmul", pu[:, 0:n], lhsT=wt[:, kc * 256 + 128:kc * 256 + 256], rhs=xn[:, kc, off:off + n],
                                                               start=(kc == 0), stop=(kc == 7),
                       r=[wt.b(), xn.b((kc, ti))], w=[pu.b()])
                st = rot("sg", sg)
                ACT("activation", out=st[:, 0:n], in_=pg[:, 0:n], func=AF.Silu, r=[pg.b()], w=[st.b()])
                DVE("tensor_tensor", out=big[:, j, off:off + n], in0=st[:, 0:n], in1=pu[:, 0:n], op=ALU.mult,
                    r=[st.b(), pu.b()], w=[big.b((j, ti))])
        down_postnorm(tiles, big, NJ, lambda m: w_d[l, f, m], DFF, gpost)

    def conv(kind, n, pp, wcol, bcol, tail, tailb, sin, sinb, sout, soutb, out_fn, out_bufs, silu):
        ci = cnt["xp"] % 2
        cnt["xp"] += 1
        ac, xt, xs_ = acc[ci], xp[ci], xps[ci]
        if kind == "T":
            ACT("activation", out=xt[:, 0:3], in_=tail, func=AF.Copy, r=[tailb], w=[xt.b()])
            ACT("activation", out=xt[:, 3:3 + n], in_=pp[:, 0:n], func=AF.Copy, r=[pp.b()], w=[xt.b()])
            DVE("tensor_copy", tail, xt[:, n:n + 3], r=[xt.b()], w=[tailb])
            DVE("tensor_scalar", ac[:, 0:n], xt[:, 3:3 + n], wcol[:, 3:4], bcol, ALU.mult, ALU.add, r=[xt.b(), pv.b()], w=[ac.b()])
            for k in (2, 1, 0):
                DVE("scalar_tensor_tensor", out=ac[:, 0:n], in0=xt[:, k:k + n], scalar=wcol[:, k:k + 1], in1=ac[:, 0:n],
                                                          op0=ALU.mult, op1=ALU.add, r=[xt.b(), ac.b(), pv.b()], w=[ac.b()])
        else:
            nm = NMETA
            DVE("memset", xt[:, 0:3], 0.0, w=[xt.b()])
            ACT("activation", out=xt[:, 3:3 + nm], in_=pp[:, 0:nm], func=AF.Copy, r=[pp.b()], w=[xt.b()])
            DVE("tensor_copy", tail, xt[:, nm:nm + 3], r=[xt.b()], w=[tailb])
            DVE("tensor_scalar", ac[:, 0:nm], xt[:, 3:3 + nm], wcol[:, 3:4], bcol, ALU.mult, ALU.add, r=[xt.b(), pv.b()], w=[ac.b()])
            for k in (2, 1, 0):
                DVE("scalar_tensor_tensor", out=ac[:, 0:nm], in0=xt[:, k:k + nm], scalar=wcol[:, k:k + 1], in1=ac[:, 0:nm],
                                                          op0=ALU.mult, op1=ALU.add, r=[xt.b(), ac.b(), pv.b()], w=[ac.b()])
            ACT("activation", out=xs_[:, :, 0:3], in_=sin, func=AF.Copy, r=[sinb], w=[xs_.b()])
            ACT("activation", out=xs_[:, :, 3:7], in_=pp[:, nm:n].rearrange("p (s t) -> p s t", t=DEC), func=AF.Copy,
                r=[pp.b()], w=[xs_.b()])
            DVE("tensor_copy", sout, xs_[:, :, 4:7], r=[xs_.b()], w=[soutb])
            acs = ac[:, nm:n].rearrange("p (s t) -> p s t", t=DEC)
            DVE("tensor_scalar", acs, xs_[:, :, 3:7], wcol[:, 3:4], bcol, ALU.mult, ALU.add, r=[xs_.b(), pv.b()], w=[ac.b()])
            for k in (2, 1, 0):
                DVE("scalar_tensor_tensor", out=acs, in0=xs_[:, :, k:k + DEC], scalar=wcol[:, k:k + 1], in1=acs,
                                                          op0=ALU.mult, op1=ALU.add, r=[xs_.b(), ac.b(), pv.b()], w=[ac.b()])
        if silu:
            ACT("activation", out=out_fn(), in_=ac[:, 0:n], func=AF.Silu, r=[ac.b()], w=out_bufs)
        return ac

    def ssd_chunk(l, ti, col0, T, init, hT, hTb):
        hTt, hTap = hT
        cols = slice(col0, col0 + T)
        xsb = [xbcT.b((c, ti)) for c in range(12)]
        p7 = ps[7]
        for kc in range(8):
            PE("matmul", p7[0:T, 0:16], lhsT=xn[:, kc, cols], rhs=wdt[:, l, kc * 16:(kc + 1) * 16],
                                         start=(kc == 0), stop=(kc == 7),
               r=[xn.b((kc, ti)), wdt.b()], w=[p7.b("a")])
        DVE("tensor_tensor", out=sm[0:T, 0:16], in0=p7[0:T, 0:16], in1=pv[0:T, l, 188:204], op=ALU.add,
            r=[p7.b("a"), pv.b()], w=[sm.b("t1")])
        ACT("activation", out=sm[0:T, 16:32], in_=sm[0:T, 0:16], func=AF.Exp, r=[sm.b("t1")], w=[sm.b("e1")])
        ACT("activation", out=sm[0:T, 32:48], in_=sm[0:T, 16:32], func=AF.Ln, bias=1.0, scale=1.0, r=[sm.b("e1")], w=[sm.b("dt")])
        DVE("tensor_tensor", out=sm[0:T, 48:64], in0=sm[0:T, 32:48], in1=pd[0:T, l, 0:16], op=ALU.mult,
            r=[sm.b("dt"), pd.b()], w=[sm.b("dta")])
        yield
        p6b = ps[6][:, :].bitcast(BF16)
        for c in range(8):
            PE("transpose", p6b[0:T, c * 128:(c + 1) * 128], xbcT[:, c, cols], identb[:, :],
               r=[xsb[c], identb.b()], w=[ps[6].b()])
        p7b = p7[:, 320:448].bitcast(BF16)
        for g in range(2):
            PE("transpose", p7b[0:T, g * 128:(g + 1) * 128], xbcT[:, 8 + g, cols], identb[:, :],
               r=[xsb[8 + g], identb.b()], w=[p7.b("bt")])
        dtb = sm[0:T, 32:48].unsqueeze(2).to_broadcast([T, 16, 64])
        DVE("tensor_tensor", out=xdt[0:T, :].rearrange("p (h d) -> p h d", d=64),
                                      in0=p6b[0:T, :].rearrange("p (h d) -> p h d", d=64), in1=dtb, op=ALU.mult,
            r=[ps[6].b(), sm.b("dt")], w=[xdt.b()])
        ACT("activation", out=Btok[0:T, :], in_=p7b[0:T, :], func=AF.Copy, r=[p7.b("bt")], w=[Btok.b()])
        yield
        R1v = R1[0:T, 0:16 * T].rearrange("p (h i) -> p h i", i=T)
        EA = DVE if "a" in _opt("KDVE", "") else POOL
        EB = DVE if "b" in _opt("KDVE", "") else POOL
        EC = DVE if "c" in _opt("KDVE", "") else POOL
        ED = DVE if "d" in _opt("KDVE", "") else POOL
        if _opt("KR1SPLIT", "0") == "1" and T == 128:
            for hh, E_ in ((0, DVE), (1, POOL)):
                E_("tensor_tensor", out=R1v[:, hh * 8:(hh + 1) * 8, :], in0=tri(T).unsqueeze(1).to_broadcast([T, 8, T]),
                   in1=sm[0:T, 48 + hh * 8:56 + hh * 8].unsqueeze(2).to_broadcast([T, 8, T]), op=ALU.mult,
                   r=[cst.b(), sm.b("dta")], w=[R1.b(hh)])
            R1r = [R1.b(0), R1.b(1), R1.b()]
        else:
            EA("tensor_tensor", out=R1v, in0=tri(T).unsqueeze(1).to_broadcast([T, 16, T]),
                                      in1=sm[0:T, 48:64].unsqueeze(2).to_broadcast([T, 16, T]), op=ALU.mult,
               r=[cst.b(), sm.b("dta")], w=[R1.b()])
            R1r = [R1.b()]
        yield
        nq = (16 * T + 511) // 512
        PE("matmul", p7[0:T, 16:32], lhsT=Um(T), rhs=sm[0:T, 48:64], start=True, stop=True, r=[cst.b(), sm.b("dta")], w=[p7.b("b")])
        PE("matmul", p7[0:T, 32:48], lhsT=tri(T), rhs=sm[0:T, 48:64], start=True, stop=True, r=[cst.b(), sm.b("dta")], w=[p7.b("c")])
        PE("matmul", p7[:, 48:64], lhsT=onesf(T), rhs=sm[0:T, 48:64], start=True, stop=True, r=[cst.b(), sm.b("dta")], w=[p7.b("d")])
        for g in range(2):
            PE("matmul", p7[0:T, 64 + g * 128:64 + g * 128 + T], lhsT=xbcT[:, 8 + g, cols], rhs=xbcT[:, 10 + g, cols],
                                       start=True, stop=True,
               r=[xsb[8 + g], xsb[10 + g]], w=[p7.b("cb")])
        for half in range((nq + 1) // 2):
            qs = [q for q in (2 * half, 2 * half + 1) if q < nq]
            for q in qs:
                w_ = min(512, 16 * T - q * 512)
                PE("matmul", ps[q % 2][0:T, 0:w_], lhsT=Um(T), rhs=R1[0:T, q * 512:q * 512 + w_], start=True, stop=True,
                   r=[cst.b()] + R1r, w=[ps[q % 2].b()])
            for q in qs:
                w_ = min(512, 16 * T - q * 512)
                ACT("activation", out=Lb[0:T, q * 512:q * 512 + w_], in_=ps[q % 2][0:T, 0:w_], func=AF.Exp,
                    r=[ps[q % 2].b()], w=[Lb.b()])
        yield
        cbv = p7[0:T, 64:320].rearrange("p (g i) -> p g i", i=128)[:, :, 0:T]
        cbmv = cbm[0:T, 0:2 * T].rearrange("p (g i) -> p g i", i=T)
        DVE("tensor_tensor", out=cbmv, in0=cbv, in1=tri(T).unsqueeze(1).to_broadcast([T, 2, T]), op=ALU.mult,
            r=[p7.b("cb"), cst.b()], w=[cbm.b()])
        scv = scT[0:T, 0:16 * T].rearrange("p (g r i) -> p g r i", g=2, i=T)
        Lv = Lb[0:T, 0:16 * T].rearrange("p (g r i) -> p g r i", g=2, i=T)
        DVE("tensor_tensor", out=scv, in0=Lv, in1=cbmv.unsqueeze(2).to_broadcast([T, 2, 8, T]), op=ALU.mult,
            r=[Lb.b(), cbm.b()], w=[scT.b()])
        ACT("activation", out=sm[0:T, 64:80], in_=p7[0:T, 16:32], func=AF.Exp, r=[p7.b("b")], w=[sm.b("wend")])
        ACT("activation", out=sm[0:T, 80:96], in_=p7[0:T, 32:48], func=AF.Exp, r=[p7.b("c")], w=[sm.b("ecum")])
        ACT("activation", out=sm[:, 96:112], in_=p7[:, 48:64], func=AF.Exp, r=[p7.b("d")], w=[sm.b("decay")])
        EB("tensor_tensor", out=xw[0:T, :].rearrange("p (h d) -> p h d", d=64),
                                      in0=xdt[0:T, :].rearrange("p (h d) -> p h d", d=64),
                                      in1=sm[0:T, 64:80].unsqueeze(2).to_broadcast([T, 16, 64]), op=ALU.mult,
            r=[xdt.b(), sm.b("wend")], w=[xw.b()])
        yield
        for h in range(16):
            pb = ps[h // 8]
            PE("matmul", pb[0:T, (h % 8) * 64:(h % 8) * 64 + 64], lhsT=scT[0:T, h * T:(h + 1) * T],
                                              rhs=xdt[0:T, h * 64:(h + 1) * 64], start=True, stop=True,
               r=[scT.b(), xdt.b()], w=[pb.b()])
        if init:
            for g in range(2):
                PE("matmul", ps[2 + g][0:T, :], lhsT=xbcT[:, 10 + g, cols], rhs=hTb[:, g * 512:(g + 1) * 512], start=True, stop=True,
                   r=[xsb[10 + g], hTbf.b()], w=[ps[2 + g].b()])
        yield
        ysrc = ydsb
        for g in range(2):
            ACT("activation", out=ydsb[0:T, g * 512:(g + 1) * 512], in_=ps[g][0:T, :], func=AF.Copy,
                r=[ps[g].b()], w=[ydsb.b(g)])
        if init:
            for g in range(2):
                DVE("tensor_tensor", out=ytok[0:T, g * 512:(g + 1) * 512].rearrange("p (h d) -> p h d", d=64),
                                                   in0=ps[2 + g][0:T, :].rearrange("p (h d) -> p h d", d=64),
                                                   in1=sm[0:T, 80 + g * 8:88 + g * 8].unsqueeze(2).to_broadcast([T, 8, 64]), op=ALU.mult,
                    r=[ps[2 + g].b(), sm.b("ecum")], w=[ytok.b(g)])
                DVE("tensor_tensor", out=ytok[0:T, g * 512:(g + 1) * 512], in0=ytok[0:T, g * 512:(g + 1) * 512],
                                                   in1=ydsb[0:T, g * 512:(g + 1) * 512], op=ALU.add,
                    r=[ytok.b(g), ydsb.b(g)], w=[ytok.b(g)])
            ysrc = ytok
        yield
        for c in range(8):
            pb = ps[(c * T) // 512]
            PE("transpose", pb[:, (c * T) % 512:(c * T) % 512 + T], ysrc[0:T, c * 128:(c + 1) * 128], ident(T),
               r=[ysrc.b(c // 4), cst.b()], w=[pb.b()])
        for g in range(2):
            PE("matmul", ps[2 + g][:, :], lhsT=Btok[0:T, g * 128:(g + 1) * 128], rhs=xw[0:T, g * 512:(g + 1) * 512],
                                       start=True, stop=True,
               r=[Btok.b(), xw.b()], w=[ps[2 + g].b()])
        yield
        if _opt("KGO", "0") == "1":
            g1, yg = g1o, ygo
            g1b, ygb = [g1o.b()], [ygo.b()]
        else:
            g1, yg = ydsb, ytok
            g1b, ygb = [ydsb.b(0), ydsb.b(1)], [ytok.b(0), ytok.b(1)]
        g1v = g1[:, 0:8 * T].rearrange("p (c t) -> p c t", t=T)
        ygv = yg[:, 0:8 * T].rearrange("p (c t) -> p c t", t=T)
        ED("tensor_tensor", out=g1v, in0=xbcT[:, 0:8, cols], in1=pv[:, l, 56:64].unsqueeze(2).to_broadcast([128, 8, T]), op=ALU.mult,
            r=xsb[0:8] + [pv.b()], w=g1b)
        nb = (8 * T + 511) // 512
        for q in range(nb):
            w_ = min(512, 8 * T - q * 512)
            DVE("tensor_tensor", out=g1[:, q * 512:q * 512 + w_], in0=g1[:, q * 512:q * 512 + w_],
                                                      in1=ps[q][:, 0:w_], op=ALU.add,
                r=g1b + [ps[q].b()], w=g1b)
        DVE("tensor_tensor", out=ygv, in0=g1v, in1=big[:, 16:24, cols], op=ALU.mult,
            r=g1b + [big.b((16 + c, ti)) for c in range(8)], w=ygb)
        ACT("activation", out=sqg[:, 0:8 * T], in_=yg[:, 0:8 * T], func=AF.Square, r=ygb, w=[sqg.b()])
        yield
        for c in range(8):
            PE("matmul", ps[6][:, 0:T], lhsT=onesb[:, :], rhs=sqg[:, c * T:(c + 1) * T], start=(c == 0), stop=(c == 7),
               r=[onesb.b(), sqg.b()], w=[ps[6].b()])
        ACT("activation", out=rsg[:, 0:T], in_=ps[6][:, 0:T], func=AF.Ln, bias=EPS, scale=1.0 / D, r=[ps[6].b()], w=[rsg.b()])
        ACT("activation", out=rrg[:, 0:T], in_=rsg[:, 0:T], func=AF.Exp, scale=-0.5, r=[rsg.b()], w=[rrg.b()])
        ED("tensor_tensor", out=ygv, in0=ygv, in1=pv[:, l, 48:56].unsqueeze(2).to_broadcast([128, 8, T]), op=ALU.mult,
            r=ygb + [pv.b()], w=ygb)
        DVE("tensor_tensor", out=big[:, 0:8, cols], in0=ygv, in1=rrg[:, 0:T].unsqueeze(1).to_broadcast([128, 8, T]), op=ALU.mult,
            r=ygb + [rrg.b()], w=[big.b((c, ti)) for c in range(8)])
        yield
        if init:
            EC("tensor_tensor", out=hTap.rearrange("p (h d) -> p h d", d=64), in0=hTap.rearrange("p (h d) -> p h d", d=64),
                                          in1=sm[:, 96:112].unsqueeze(2).to_broadcast([128, 16, 64]), op=ALU.mult,
                r=[hTt.b(l), sm.b("decay")], w=[hTt.b(l)])
            for g in range(2):
                DVE("tensor_tensor", out=hTap[:, g * 512:(g + 1) * 512], in0=hTap[:, g * 512:(g + 1) * 512],
                                                   in1=ps[2 + g][:, :], op=ALU.add,
                    r=[hTt.b(l), ps[2 + g].b()], w=[hTt.b(l)])
        else:
            for g in range(2):
                ACT("activation", out=hTap[:, g * 512:(g + 1) * 512], in_=ps[2 + g][:, :], func=AF.Copy,
                    r=[ps[2 + g].b()], w=[hTt.b(l)])
        ACT("activation", out=hTb[:, :], in_=hTap, func=AF.Copy, r=[hTt.b(l)], w=[hTbf.b()])
        yield

    def ssd_S(l, ti, off):
        T = STILE
        cols = slice(off, off + T)
        xsb = [xbcT.b((c, ti)) for c in range(12)]
        p7 = ps[7]
        triB = cst2[0:T, 0:T]
        UB = cst2[0:T, 128:128 + T]
        blk = cst2[0:T, 256:256 + 17]
        c2 = cst2.b()
        for kc in range(8):
            PE("matmul", p7[0:T, 0:16], lhsT=xn[:, kc, cols], rhs=wdt[:, l, kc * 16:(kc + 1) * 16], start=(kc == 0), stop=(kc == 7),
               r=[xn.b((kc, ti)), wdt.b()], w=[p7.b()])
        DVE("tensor_tensor", out=sm[0:T, 0:16], in0=p7[0:T, 0:16], in1=pv[0:T, l, 188:204], op=ALU.add, r=[p7.b(), pv.b()], w=[sm.b("t1")])
        ACT("activation", out=sm[0:T, 16:32], in_=sm[0:T, 0:16], func=AF.Exp, r=[sm.b("t1")], w=[sm.b("e1")])
        ACT("activation", out=sm[0:T, 32:48], in_=sm[0:T, 16:32], func=AF.Ln, bias=1.0, scale=1.0, r=[sm.b("e1")], w=[sm.b("dt")])
        DVE("tensor_tensor", out=sm[0:T, 48:64], in0=sm[0:T, 32:48], in1=pd[0:T, l, 0:16], op=ALU.mult, r=[sm.b("dt"), pd.b()], w=[sm.b("dta")])
        yield
        p6b = ps[6][:, :].bitcast(BF16)
        for c in range(8):
            PE("transpose", p6b[0:T, c * 128:(c + 1) * 128], xbcT[:, c, cols], identb[:, :], r=[xsb[c], identb.b()], w=[ps[6].b()])
        p7b = p7[:, 192:320].bitcast(BF16)
        for g in range(2):
            PE("transpose", p7b[0:T, g * 128:(g + 1) * 128], xbcT[:, 8 + g, cols], identb[:, :], r=[xsb[8 + g], identb.b()], w=[p7.b()])
        DVE("tensor_tensor", out=xdt[0:T, :].rearrange("p (h d) -> p h d", d=64), in0=p6b[0:T, :].rearrange("p (h d) -> p h d", d=64),
            in1=sm[0:T, 32:48].unsqueeze(2).to_broadcast([T, 16, 64]), op=ALU.mult, r=[ps[6].b(), sm.b("dt")], w=[xdt.b()])
        ACT("activation", out=Btok[0:T, :], in_=p7b[0:T, :], func=AF.Copy, r=[p7.b()], w=[Btok.b()])
        dtaE = ytok
        POOL("tensor_copy", ytok[0:T, :].rearrange("p (h d) -> p h d", d=64), sm[0:T, 48:64].unsqueeze(2).to_broadcast([T, 16, 64]),
             r=[sm.b("dta")], w=[ytok.b(0), ytok.b(1)])
        yield
        R1v = R1[0:T, 0:16 * T].rearrange("p (h i) -> p h i", i=T)
        POOL("tensor_tensor", out=R1v, in0=triB.unsqueeze(1).to_broadcast([T, 16, T]),
             in1=sm[0:T, 48:64].unsqueeze(2).to_broadcast([T, 16, T]), op=ALU.mult, r=[c2, sm.b("dta")], w=[R1.b()])
        yield
        PE("matmul", p7[0:T, 16:32], lhsT=UB, rhs=sm[0:T, 48:64], start=True, stop=True, r=[c2, sm.b("dta")], w=[p7.b()])
        for g in range(2):
            PE("matmul", p7[0:T, 32 + g * T:32 + (g + 1) * T], lhsT=xbcT[:, 8 + g, cols], rhs=xbcT[:, 10 + g, cols], start=True, stop=True,
               r=[xsb[8 + g], xsb[10 + g]], w=[p7.b()])
        for c in range(8):
            PE("matmul", p7[:, 320 + c * 17:320 + (c + 1) * 17], lhsT=ytok[0:T, c * 128:(c + 1) * 128], rhs=blk, start=True, stop=True,
               r=[ytok.b(c // 4), c2], w=[p7.b()])
        nq = (16 * T + 511) // 512
        for half in range((nq + 1) // 2):
            qs = [q for q in (2 * half, 2 * half + 1) if q < nq]
            for q in qs:
                w_ = min(512, 16 * T - q * 512)
                PE("matmul", ps[q % 2][0:T, 0:w_], lhsT=UB, rhs=R1[0:T, q * 512:q * 512 + w_], start=True, stop=True,
                   r=[c2, R1.b()], w=[ps[q % 2].b()])
            for q in qs:
                w_ = min(512, 16 * T - q * 512)
                ACT("activation", out=Lb[0:T, q * 512:q * 512 + w_], in_=ps[q % 2][0:T, 0:w_], func=AF.Exp, r=[ps[q % 2].b()], w=[Lb.b()])
        yield
        cbv = p7[0:T, 32:32 + 2 * T].rearrange("p (g i) -> p g i", i=T)
        cbmv = cbm[0:T, 0:2 * T].rearrange("p (g i) -> p g i", i=T)
        DVE("tensor_tensor", out=cbmv, in0=cbv, in1=triB.unsqueeze(1).to_broadcast([T, 2, T]), op=ALU.mult, r=[p7.b(), c2], w=[cbm.b()])
        scv = scT[0:T, 0:16 * T].rearrange("p (g r i) -> p g r i", g=2, i=T)
        Lv = Lb[0:T, 0:16 * T].rearrange("p (g r i) -> p g r i", g=2, i=T)
        DVE("tensor_tensor", out=scv, in0=Lv, in1=cbmv.unsqueeze(2).to_broadcast([T, 2, 8, T]), op=ALU.mult, r=[Lb.b(), cbm.b()], w=[scT.b()])
        ACT("activation", out=sm[0:T, 64:80], in_=p7[0:T, 16:32], func=AF.Exp, r=[p7.b()], w=[sm.b("wend")])
        decN = R1[:, 1024:1024 + 136]
        ACT("activation", out=decN, in_=p7[:, 320:456], func=AF.Exp, r=[p7.b(), R1.b()], w=[R1.b()])
        POOL("tensor_tensor", out=xw[0:T, :].rearrange("p (h d) -> p h d", d=64), in0=xdt[0:T, :].rearrange("p (h d) -> p h d", d=64),
             in1=sm[0:T, 64:80].unsqueeze(2).to_broadcast([T, 16, 64]), op=ALU.mult, r=[xdt.b(), sm.b("wend")], w=[xw.b()])
        yield
        for h in range(16):
            pb = ps[h // 8]
            PE("matmul", pb[0:T, (h % 8) * 64:(h % 8) * 64 + 64], lhsT=scT[0:T, h * T:(h + 1) * T], rhs=xdt[0:T, h * 64:(h + 1) * 64],
               start=True, stop=True, r=[scT.b(), xdt.b()], w=[pb.b()])
        for g in range(2):
            ACT("activation", out=ydsb[0:T, g * 512:(g + 1) * 512], in_=ps[g][0:T, :], func=AF.Copy, r=[ps[g].b()], w=[ydsb.b(g)])
        yield
        for c in range(8):
            pb = ps[c // 4]
            PE("transpose", pb[:, (c % 4) * 128:(c % 4) * 128 + T], ydsb[0:T, c * 128:(c + 1) * 128], ident(T),
               r=[ydsb.b(c // 4), cst.b()], w=[pb.b()])
        for c in range(8):
            pb = ps[2 + c // 4]
            PE("matmul", pb[:, (c % 4) * 128:(c % 4) * 128 + T], lhsT=ytok[0:T, c * 128:(c + 1) * 128], rhs=triB, start=True, stop=True,
               r=[ytok.b(c // 4), c2], w=[pb.b()])
        ecT = R1[:, 0:1024].rearrange("p (c t) -> p c t", t=128)[:, :, 0:T]
        for g in range(2):
            ACT("activation", out=R1[:, g * 512:(g + 1) * 512].rearrange("p (c t) -> p c t", t=128)[:, :, 0:T],
                in_=ps[2 + g][:, :].rearrange("p (c t) -> p c t", t=128)[:, :, 0:T], func=AF.Exp, r=[ps[2 + g].b(), R1.b()], w=[R1.b()])
        yield
        g1v = hTs[:, 0:8 * T].rearrange("p (c t) -> p c t", t=T)
        POOL("tensor_tensor", out=g1v, in0=xbcT[:, 0:8, cols], in1=pv[:, l, 56:64].unsqueeze(2).to_broadcast([128, 8, T]), op=ALU.mult,
             r=xsb[0:8] + [pv.b()], w=[hTs.b()])
        for g in range(2):
            DVE("tensor_tensor", out=g1v[:, g * 4:(g + 1) * 4, :], in0=g1v[:, g * 4:(g + 1) * 4, :],
                in1=ps[g][:, :].rearrange("p (c t) -> p c t", t=128)[:, :, 0:T], op=ALU.add, r=[hTs.b(), ps[g].b()], w=[hTs.b()])
        DVE("tensor_scalar", BtokM[0:T, :], Btok[0:T, :], cst2[0:T, 256:257], None, ALU.mult, r=[Btok.b(), c2], w=[BtokM.b()])
        for g in range(2):
            PE("matmul", ps[2 + g][:, :], lhsT=BtokM[0:T, g * 128:(g + 1) * 128], rhs=xw[0:T, g * 512:(g + 1) * 512], start=True, stop=True,
               r=[BtokM.b(), xw.b()], w=[ps[2 + g].b()])
        for g in range(2):
            ACT("activation", out=hT_p[:, l, g * 512:(g + 1) * 512], in_=ps[2 + g][:, :], func=AF.Copy, r=[ps[2 + g].b()], w=[hT_p.b(l)])
        yield
        hb2 = [hTbf[:, :], ydsb[:, 0:512].bitcast(BF16)]
        hb2b = [[hTbf.b()], [ydsb.b(0)]]
        for s_ in range(NSAMP):
            nt = nat[s_ % 2]
            hb, hbb = hb2[s_ % 2], hb2b[s_ % 2]
            sc = off + NMETA + DEC * s_
            DMA("sp", nt[:, :, :], st_ssd[l, s_].rearrange("(c p) n -> p c n", p=128), w=[nt.b()])
            for c in range(8):
                PE("transpose", ps[2 + c // 4][:, (c % 4) * 128:(c % 4) * 128 + 128], nt[:, c, :], ident(), r=[nt.b(), cst.b()], w=[ps[2 + c // 4].b()])
            for g in range(2):
                ACT("activation", out=hb[:, g * 512:(g + 1) * 512], in_=ps[2 + g][:, :], func=AF.Copy, r=[ps[2 + g].b()], w=hbb)
            for c in range(8):
                PE("matmul", ps[6][:, c * 64 + DEC * s_:c * 64 + DEC * s_ + DEC], lhsT=hb[:, c * 128:(c + 1) * 128],
                   rhs=xbcT[:, 10 + c // 4, sc:sc + DEC], start=True, stop=True, r=hbb + [xsb[10 + c // 4]], w=[ps[6].b()])
            DVE("tensor_scalar", BtokM[0:T, :], Btok[0:T, :], cst2[0:T, 257 + s_:258 + s_], None, ALU.mult, r=[Btok.b(), c2], w=[BtokM.b()])
            for c in range(8):
                PE("matmul", ps[c // 4][:, (c % 4) * 128:(c % 4) * 128 + 128], lhsT=xw[0:T, c * 128:(c + 1) * 128],
                   rhs=BtokM[0:T, (c // 4) * 128:(c // 4) * 128 + 128], start=True, stop=True, r=[xw.b(), BtokM.b()], w=[ps[c // 4].b()])
            dsv = R1[:, 1024:1024 + 136].rearrange("p (c b) -> p c b", b=17)[:, :, 1 + s_:2 + s_].to_broadcast([128, 8, 128])
            POOL("tensor_tensor", out=nt[:, :, :], in0=nt[:, :, :], in1=dsv, op=ALU.mult, r=[nt.b(), R1.b()], w=[nt.b()])
            for g in range(2):
                DVE("tensor_tensor", out=nt[:, g * 4:(g + 1) * 4, :], in0=nt[:, g * 4:(g + 1) * 4, :],
                    in1=ps[g][:, :].rearrange("p (c n) -> p c n", n=128), op=ALU.add, r=[nt.b(), ps[g].b()], w=[nt.b()])
            DMA("sp", o_ssd_s[l, s_].rearrange("(c p) n -> p c n", p=128), nt[:, :, :], r=[nt.b()])
            yield
        yo = ps[6][:, :].rearrange("p (c t) -> p c t", t=64)
        g1s = g1v[:, :, NMETA:T]
        tv = ytok[:, 0:512].rearrange("p (c t) -> p c t", t=64)
        DVE("tensor_tensor", out=tv, in0=yo, in1=ecT[:, :, NMETA:T], op=ALU.mult, r=[ps[6].b(), R1.b()], w=[ytok.b(0)])
        DVE("tensor_tensor", out=g1s, in0=g1s, in1=tv, op=ALU.add, r=[hTs.b(), ytok.b(0)], w=[hTs.b()])
        DVE("tensor_tensor", out=g1v, in0=g1v, in1=big[:, 16:24, cols], op=ALU.mult,
            r=[hTs.b()] + [big.b((16 + c, ti)) for c in range(8)], w=[hTs.b()])
        ACT("activation", out=sqg[:, 0:8 * T], in_=hTs[:, 0:8 * T], func=AF.Square, r=[hTs.b()], w=[sqg.b()])
        yield
        for c in range(8):
            PE("matmul", ps[2][:, 0:T], lhsT=onesb[:, :], rhs=sqg[:, c * T:(c + 1) * T], start=(c == 0), stop=(c == 7),
               r=[onesb.b(), sqg.b()], w=[ps[2].b()])
        ACT("activation", out=rsg[:, 0:T], in_=ps[2][:, 0:T], func=AF.Ln, bias=EPS, scale=1.0 / D, r=[ps[2].b()], w=[rsg.b()])
        ACT("activation", out=rrg[:, 0:T], in_=rsg[:, 0:T], func=AF.Exp, scale=-0.5, r=[rsg.b()], w=[rrg.b()])
        POOL("tensor_tensor", out=g1v, in0=g1v, in1=pv[:, l, 48:56].unsqueeze(2).to_broadcast([128, 8, T]), op=ALU.mult,
             r=[hTs.b(), pv.b()], w=[hTs.b()])
        DVE("tensor_tensor", out=big[:, 0:8, cols], in0=g1v, in1=rrg[:, 0:T].unsqueeze(1).to_broadcast([128, 8, T]), op=ALU.mult,
            r=[hTs.b(), rrg.b()], w=[big.b((c, ti)) for c in range(8)])
        ACT("activation", out=hTbf[:, :], in_=hT_p[:, l, :], func=AF.Copy, r=[hT_p.b(l)], w=[hTbf.b()])
        yield

    def interleave(*gens):
        gens = list(gens)
        while gens:
            for g in list(gens):
                try:
                    next(g)
                except StopIteration:
                    gens.remove(g)

    def mixer(l, gi, tiles):
        last_group = gi == len(GROUPS) - 1
        prenorm(tiles, pvc(l, 16, 24))
        hasS = tiles[0][0] == "S"
        if hasS:
            DMA("sp", sin_ssdc[:, :, :, :], st_ssdc[l], w=[sin_ssdc.b()])
            DMA("sp", sin_lruc[:, :, :, :], st_lruc[l], w=[sin_lruc.b()])
            DMA("sp", sin_lru[:, :, :], st_lru[l], w=[sin_lru.b()])
        for blk in range(20):
            wt = next_ws()
            for ti, (kind, off, n) in enumerate(tiles):
                pp = ps[cnt["bank"] % 4]
                cnt["bank"] += 1
                for kc in range(8):
                    PE("matmul", pp[:, 0:n], lhsT=wt[:, kc * 128:(kc + 1) * 128], rhs=xn[:, kc, off:off + n],
                                                               start=(kc == 0), stop=(kc == 7),
                       r=[wt.b(), xn.b((kc, ti))], w=[pp.b()])
                if blk < 8:
                    ACT("activation", out=big[:, 16 + blk, off:off + n], in_=pp[:, 0:n], func=AF.Silu,
                        r=[pp.b()], w=[big.b((16 + blk, ti))])
                else:
                    c = blk - 8
                    conv(kind, n, pp, pv[:, l, 140 + 4 * c:144 + 4 * c], pv[:, l, 128 + c:129 + c],
                         tail_xbc[:, l, c, :], tail_xbc.b((l, c)),
                         sin_ssdc[:, c, :, :], sin_ssdc.b(), sout_ssdc[:, c, :, :], sout_ssdc.b(),
                         lambda c=c, off=off, n=n: xbcT[:, c, off:off + n], [xbcT.b((c, ti))], True)
        def m2_gen():
          for ti, (kind, off, n) in enumerate(tiles):
            if kind == "S":
                yield from ssd_S(l, ti, off)
            else:
                if ti == 0:
                    ACT("activation", out=hTbf[:, :], in_=hT_p[:, l, :], func=AF.Copy, r=[hT_p.b(l)], w=[hTbf.b()])
                for c4 in range(n // 128):
                    yield from ssd_chunk(l, ti, off + 128 * c4, 128, True, (hT_p, hT_p[:, l, :]), hTbf)
          if hasS:
            DMA("sp", o_ssdc_s[l], sout_ssdc[:, :, :, :], r=[sout_ssdc.b()])
          if last_group:
            DMA("sp", o_ssd_p[l], hT_p[:, l, :], r=[hT_p.b(l)])
            DMA("sp", o_ssdc_p[l], tail_xbc[:, l, :, :], r=[tail_xbc.b((l, c)) for c in range(12)])

        def m3_gen():
          for c in range(8):
            wt = next_ws()
            wa = rot("wab", wab)
            DMA("pool", wa[:, :], w_ab[l, c], w=[wa.b()])
            for ti, (kind, off, n) in enumerate(tiles):
                pg, px = ps[4], ps[5]
                for kc in range(8):
                    PE("matmul", pg[:, 0:n], lhsT=wt[:, kc * 256:kc * 256 + 128], rhs=xn[:, kc, off:off + n],
                                                               start=(kc == 0), stop=(kc == 7),
                       r=[wt.b(), xn.b((kc, ti))], w=[pg.b()])
                for kc in range(8):
                    PE("matmul", px[:, 0:n], lhsT=wt[:, kc * 256 + 128:kc * 256 + 256], rhs=xn[:, kc, off:off + n],
                                                               start=(kc == 0), stop=(kc == 7),
                       r=[wt.b(), xn.b((kc, ti))], w=[px.b()])
                yield
                gl, g2 = lt["gl"], lt["g2"]
                ACT("activation", out=g2[:, 0:n], in_=pg[:, 0:n], func=AF.Square, r=[pg.b()], w=[g2.b()])
                DVE("tensor_scalar", g2[:, 0:n], g2[:, 0:n], 0.044715, 1.0, ALU.mult, ALU.add, r=[g2.b()], w=[g2.b()])
                DVE("tensor_tensor", out=g2[:, 0:n], in0=g2[:, 0:n], in1=pg[:, 0:n], op=ALU.mult, r=[g2.b(), pg.b()], w=[g2.b()])
                ACT("activation", out=g2[:, 0:n], in_=g2[:, 0:n], func=AF.Exp, scale=-1.5957691216057308, r=[g2.b()], w=[g2.b()])
                ACT("activation", out=g2[:, 0:n], in_=g2[:, 0:n], func=AF.Ln, bias=1.0, scale=1.0, r=[g2.b()], w=[g2.b()])
                ACT("activation", out=g2[:, 0:n], in_=g2[:, 0:n], func=AF.Exp, scale=-1.0, r=[g2.b()], w=[g2.b()])
                DVE("tensor_tensor", out=gl[:, 0:n], in0=g2[:, 0:n], in1=pg[:, 0:n], op=ALU.mult, r=[g2.b(), pg.b()], w=[gl.b()])
                yield
                ac = conv(kind, n, px, pv[:, l, 96 + 4 * c:100 + 4 * c], pv[:, l, 88 + c:89 + c],
                          tail_xr[:, l, c, :], tail_xr.b((l, c)),
                          sin_lruc[:, c, :, :], sin_lruc.b(), sout_lruc[:, c, :, :], sout_lruc.b(),
                          None, None, False)
                ACT("activation", out=xrbf[:, 0:n], in_=ac[:, 0:n], func=AF.Copy, r=[ac.b()], w=[xrbf.b()])
                yield
                pr, pi = ps[5], ps[4]
                PE("matmul", pr[:, 0:n], lhsT=wa[:, 0:128], rhs=xrbf[:, 0:n], start=True, stop=True,
                   r=[wa.b(), xrbf.b()], w=[pr.b()])
                PE("matmul", pi[:, 0:n], lhsT=wa[:, 128:256], rhs=xrbf[:, 0:n], start=True, stop=True,
                   r=[wa.b(), xrbf.b()], w=[pi.b()])
                r_, ig, a_, a2, mu, bt, hs = lt["r"], lt["ig"], lt["a"], lt["a2"], lt["mu"], lt["bt"], lt["hs"]
                ACT("activation", out=r_[:, 0:n], in_=pr[:, 0:n], func=AF.Exp, bias=pd[:, l, 64 + c:65 + c], scale=-1.0,
                    r=[pr.b(), pd.b()], w=[r_.b()])
                ACT("activation", out=r_[:, 0:n], in_=r_[:, 0:n], func=AF.Ln, bias=1.0, scale=1.0, r=[r_.b()], w=[r_.b()])
                ACT("activation", out=r_[:, 0:n], in_=r_[:, 0:n], func=AF.Exp, scale=-1.0, r=[r_.b()], w=[r_.b()])
                ACT("activation", out=ig[:, 0:n], in_=pi[:, 0:n], func=AF.Exp, bias=pd[:, l, 72 + c:73 + c], scale=-1.0,
                    r=[pi.b(), pd.b()], w=[ig.b()])
                ACT("activation", out=ig[:, 0:n], in_=ig[:, 0:n], func=AF.Ln, bias=1.0, scale=1.0, r=[ig.b()], w=[ig.b()])
                ACT("activation", out=ig[:, 0:n], in_=ig[:, 0:n], func=AF.Exp, scale=-1.0, r=[ig.b()], w=[ig.b()])
                ACT("activation", out=a_[:, 0:n], in_=r_[:, 0:n], func=AF.Exp, scale=pd[:, l, 16 + c:17 + c], r=[r_.b(), pd.b()], w=[a_.b()])
                ACT("activation", out=a2[:, 0:n], in_=r_[:, 0:n], func=AF.Exp, scale=pd[:, l, 24 + c:25 + c], r=[r_.b(), pd.b()], w=[a2.b()])
                ACT("activation", out=mu[:, 0:n], in_=a2[:, 0:n], func=AF.Ln, bias=1.0, scale=-1.0, r=[a2.b()], w=[mu.b()])
                ACT("activation", out=mu[:, 0:n], in_=mu[:, 0:n], func=AF.Exp, scale=0.5, r=[mu.b()], w=[mu.b()])
                DVE("tensor_tensor", out=bt[:, 0:n], in0=ig[:, 0:n], in1=mu[:, 0:n], op=ALU.mult, r=[ig.b(), mu.b()], w=[bt.b()])
                DVE("tensor_tensor", out=bt[:, 0:n], in0=bt[:, 0:n], in1=ac[:, 0:n], op=ALU.mult, r=[bt.b(), ac.b()], w=[bt.b()])
                if kind == "S":
                    a0 = a_[:, NMETA:n].rearrange("p (s t) -> p s t", t=DEC)[:, :, 0:1]
                    b0 = bt[:, NMETA:n].rearrange("p (s t) -> p s t", t=DEC)[:, :, 0:1]
                    DVE("tensor_tensor", out=tmp16[:, :].unsqueeze(2), in0=a0, in1=sin_lru[:, c, :].unsqueeze(2), op=ALU.mult,
                        r=[a_.b(), sin_lru.b()], w=[tmp16.b()])
                    DVE("tensor_tensor", out=b0, in0=b0, in1=tmp16[:, :].unsqueeze(2), op=ALU.add, r=[bt.b(), tmp16.b()], w=[bt.b()])
                    DVE("memset", a0, 0.0, r=[tmp16.b()], w=[a_.b()])
                    DVE("tensor_tensor_scan", out=hs[:, 0:n], data0=a_[:, 0:n], data1=bt[:, 0:n], initial=0.0, op0=ALU.mult, op1=ALU.add,
                        r=[a_.b(), bt.b()], w=[hs.b()])
                    DVE("tensor_copy", lru_carry[:, l, c:c + 1], hs[:, NMETA - 1:NMETA], r=[hs.b()], w=[lru_carry.b(l)])
                    DVE("tensor_copy", sout_lru[:, c, :].unsqueeze(2), hs[:, NMETA:n].rearrange("p (s t) -> p s t", t=DEC)[:, :, DEC - 1:DEC],
                         r=[hs.b()], w=[sout_lru.b()])
                else:
                    DVE("tensor_tensor_scan", out=hs[:, 0:n], data0=a_[:, 0:n], data1=bt[:, 0:n], initial=lru_carry[:, l, c:c + 1],
                                                       op0=ALU.mult, op1=ALU.add,
                        r=[a_.b(), bt.b(), lru_carry.b(l)], w=[hs.b()])
                    DVE("tensor_copy", lru_carry[:, l, c:c + 1], hs[:, n - 1:n], r=[hs.b()], w=[lru_carry.b(l)])
                DVE("tensor_tensor", out=big[:, 8 + c, off:off + n], in0=hs[:, 0:n], in1=gl[:, 0:n], op=ALU.mult,
                    r=[hs.b(), gl.b()], w=[big.b((8 + c, ti))])
        interleave(m2_gen(), m3_gen())
        if hasS:
            DMA("sp", o_lruc_s[l], sout_lruc[:, :, :, :], r=[sout_lruc.b()])
            DMA("sp", o_lru_s[l], sout_lru[:, :, :], r=[sout_lru.b()])
        if last_group:
            DMA("sp", o_lru_p[l], lru_carry[:, l, :], r=[lru_carry.b(l)])
            DMA("sp", o_lruc_p[l], tail_xr[:, l, :, :], r=[tail_xr.b((l, c)) for c in range(8)])
        down_postnorm(tiles, big, 16, lambda m: w_out[l, m], 2048, pvc(l, 24, 32))

    DMA("pool", wdt[:, :, :], w_dt.rearrange("l p n -> p l n"), w=[wdt.b()])
    xin_v = xT_in.rearrange("(kc p) n -> p kc n", p=128)
    yout_v = yT_out.rearrange("(kc p) n -> p kc n", p=128)
    for gi, (gbase, tiles) in enumerate(GROUPS[:int(_opt("KGROUPS", "99"))]):
        NT = sum(t[2] for t in tiles)
        CUR_TILES[0] = tiles
        for ti, (kind, off, n) in enumerate(tiles):
            DMA("sp", x[:, :, off:off + n], xin_v[:, :, gbase + off:gbase + off + n], w=[x.b((kc, ti)) for kc in range(8)])
        for l in range(2):
            plan_ffn(l, 0)
            if stop_after == "ffn1":
                break
            plan_mixer(l)
            if stop_after == "mix":
                break
            plan_ffn(l, 1)
            if stop_after == "l0":
                break
        for l in range(2):
            PHASE[0] = f"g{gi}l{l}ffn1"
            ffn(l, 0, tiles)
            if stop_after == "ffn1":
                break
            PHASE[0] = f"g{gi}l{l}mix"
            mixer(l, gi, tiles)
            if stop_after == "mix":
                break
            PHASE[0] = f"g{gi}l{l}ffn2"
            ffn(l, 1, tiles)
            if stop_after == "l0":
                break
        for ti, (kind, off, n) in enumerate(tiles):
            DMA("sp", yout_v[:, :, gbase + off:gbase + off + n], x[:, :, off:off + n], r=[x.b((kc, ti)) for kc in range(8)])

    emit(nc, stack, S)
    stack.close()
    return nc


def _consts():
    c = np.zeros((128, 512), np.float32)
    i = np.arange(128)
    c[:, 0:128] = np.eye(128, dtype=np.float32)
    c[:, 128:256] = (i[:, None] <= i[None, :]).astype(np.float32)
    c[:, 256:384] = (i[:, None] > i[None, :]).astype(np.float32)
    c[:, 384:512] = 1.0
    return c


def _consts2():
    c = np.zeros((128, 384), np.float32)
    bid = np.full(128, -1)
    bid[0:NMETA] = 0
    for s_ in range(NSAMP):
        bid[NMETA + DEC * s_:NMETA + DEC * (s_ + 1)] = 1 + s_
    i = np.arange(128)
    same = (bid[:, None] == bid[None, :]) & (bid[:, None] >= 0)
    c[:, 0:128] = ((i[:, None] <= i[None, :]) & same).astype(np.float32)
    c[:, 128:256] = ((i[:, None] > i[None, :]) & same).astype(np.float32)
    for b in range(17):
        c[:, 256 + b] = (bid == b).astype(np.float32)
    return c


def _col(v):
    return np.ascontiguousarray(v.reshape(-1, 128).T)


def prep_shared(inp):
    f = lambda a: np.asarray(a, dtype=np.float32)
    sh = {"cst": _consts(), "cst2": _consts2()}
    pv = np.zeros((2, 128, NPV), np.float32)
    for l in range(2):
        pv[l, :, 0:8] = _col(f(inp["ffn1_pre_g"])[l])
        pv[l, :, 8:16] = _col(f(inp["ffn1_post_g"])[l])
        pv[l, :, 16:24] = _col(f(inp["mix_pre_g"])[l])
        pv[l, :, 24:32] = _col(f(inp["mix_post_g"])[l])
        pv[l, :, 32:40] = _col(f(inp["ffn2_pre_g"])[l])
        pv[l, :, 40:48] = _col(f(inp["ffn2_post_g"])[l])
        pv[l, :, 48:56] = _col(f(inp["ssd_norm_g"])[l])
        pv[l, :, 56:64] = _col(np.repeat(f(inp["ssd_d"])[l], 64))
        pv[l, :, 64:72] = _col(f(inp["lru_ba"])[l])
        pv[l, :, 72:80] = _col(f(inp["lru_bx"])[l])
        pv[l, :, 80:88] = _col(f(inp["lru_lambda"])[l])
        pv[l, :, 88:96] = _col(f(inp["lru_conv_b"])[l])
        lw = f(inp["lru_conv_w"])[l]
        for k in range(4):
            pv[l, :, 96 + k:128:4] = _col(lw[k])
        pv[l, :, 128:140] = _col(f(inp["ssd_conv_b"])[l])
        sw = f(inp["ssd_conv_w"])[l]
        for k in range(4):
            pv[l, :, 140 + k:188:4] = _col(sw[k])
        pv[l, :, 188:204] = f(inp["ssd_dt_bias"])[l][None, :]
        pv[l, :, 204:220] = f(inp["ssd_a_log"])[l][None, :]
    sh["pv"] = pv
    w_gu = np.empty((2, 2, NJ, 128, 8, 256), np.float32)
    w_d = np.empty((2, 2, 8, 128, NJ, 128), np.float32)
    for fi, nm in enumerate(["ffn1", "ffn2"]):
        wg = f(inp[nm + "_wg"]).reshape(2, 8, 128, NJ, 128)
        wu = f(inp[nm + "_wu"]).reshape(2, 8, 128, NJ, 128)
        w_gu[:, fi, :, :, :, 0:128] = wg.transpose(0, 3, 2, 1, 4)
        w_gu[:, fi, :, :, :, 128:256] = wu.transpose(0, 3, 2, 1, 4)
        wd = f(inp[nm + "_wd"]).reshape(2, NJ, 128, 8, 128)
        w_d[:, fi] = wd.transpose(0, 3, 2, 1, 4)
    sh["w_gu"] = w_gu.reshape(2, 2, NJ, 128, 2048)
    sh["w_d"] = w_d.reshape(2, 2, 8, 128, DFF)
    win = f(inp["w_in"]).reshape(2, 8, 128, 4624)
    zx = win[..., 0:2560].reshape(2, 8, 128, 20, 128)
    sh["w_zx"] = np.ascontiguousarray(zx.transpose(0, 3, 2, 1, 4)).reshape(2, 20, 128, 1024)
    sh["w_dt"] = np.ascontiguousarray(win[..., 2560:2576].transpose(0, 2, 1, 3)).reshape(2, 128, 128)
    gate = win[..., 2576:3600].reshape(2, 8, 128, 8, 128)
    xr = win[..., 3600:4624].reshape(2, 8, 128, 8, 128)
    wl = np.empty((2, 8, 128, 8, 256), np.float32)
    wl[..., 0:128] = gate.transpose(0, 3, 2, 1, 4)
    wl[..., 128:256] = xr.transpose(0, 3, 2, 1, 4)
    sh["w_lru"] = wl.reshape(2, 8, 128, 2048)
    wab = np.zeros((2, 8, 128, 256), np.float32)
    wa, wx = f(inp["lru_wa"]), f(inp["lru_wx"])
    for c in range(8):
        for h in range(2):
            wab[:, c, h * 64:(h + 1) * 64, h * 64:(h + 1) * 64] = wa[:, 2 * c + h]
            wab[:, c, h * 64:(h + 1) * 64, 128 + h * 64:128 + (h + 1) * 64] = wx[:, 2 * c + h]
    sh["w_ab"] = wab
    wo = f(inp["w_out"]).reshape(2, 16, 128, 8, 128)
    sh["w_out"] = np.ascontiguousarray(wo.transpose(0, 3, 2, 1, 4)).reshape(2, 8, 128, 2048)
    return sh


def prep_core(inp, b, sh):
    f = lambda a: np.asarray(a, dtype=np.float32)
    m = dict(sh)
    s0, s1 = NSAMP * b, NSAMP * (b + 1)
    rows = np.concatenate([f(inp["meta_tokens"]), f(inp["x_sample"])[s0:s1].reshape(NSAMP * DEC, D), f(inp["x_prompt"])[b]], axis=0)
    m["xT_in"] = np.ascontiguousarray(rows.T)
    m["st_ssd"] = np.ascontiguousarray(f(inp["state_ssd"])[:, s0:s1].reshape(2, NSAMP, D, 128))
    sc = f(inp["state_ssd_conv"])[:, s0:s1]
    m["st_ssdc"] = np.ascontiguousarray(sc.reshape(2, NSAMP, 3, 12, 128).transpose(0, 4, 3, 1, 2))
    m["st_lru"] = np.ascontiguousarray(f(inp["state_lru"])[:, s0:s1].reshape(2, NSAMP, 8, 128).transpose(0, 3, 2, 1))
    lc = f(inp["state_lru_conv"])[:, s0:s1]
    m["st_lruc"] = np.ascontiguousarray(lc.reshape(2, NSAMP, 3, 8, 128).transpose(0, 4, 3, 1, 2))
    return m


_NC_CACHE = {}


def kernel(**inputs):
    n = 8
    sh = prep_shared(inputs)
    in_maps = [prep_core(inputs, b, sh) for b in range(n)]
    if "nc" not in _NC_CACHE:
        _NC_CACHE["nc"] = build_program()
    nc = _NC_CACHE["nc"]
    res = run_bass_kernel_spmd(nc, in_maps, core_ids=list(range(n)))
    return assemble(res.results, n)


def assemble(results, n):
    B = n
    y_prompt = np.empty((B, SEQ, D), np.float32)
    y_sample = np.empty((B * NSAMP, DEC, D), np.float32)
    ssd_p = np.empty((2, B, 16, 64, 128), np.float32)
    ssdc_p = np.empty((2, B, 3, 1536), np.float32)
    lru_p = np.empty((2, B, D), np.float32)
    lruc_p = np.empty((2, B, 3, D), np.float32)
    ssd_s = np.empty((2, B * NSAMP, 16, 64, 128), np.float32)
    ssdc_s = np.empty((2, B * NSAMP, 3, 1536), np.float32)
    lru_s = np.empty((2, B * NSAMP, D), np.float32)
    lruc_s = np.empty((2, B * NSAMP, 3, D), np.float32)
    for b in range(B):
        r = results[b]
        yT = r["yT_out"]
        y_prompt[b] = yT[:, STILE:].T
        y_sample[b * NSAMP:(b + 1) * NSAMP] = yT[:, NMETA:STILE].T.reshape(NSAMP, DEC, D)
        ssd_p[:, b] = r["o_ssd_p"].transpose(0, 2, 1).reshape(2, 16, 64, 128)
        ssdc_p[:, b] = r["o_ssdc_p"].transpose(0, 3, 2, 1).reshape(2, 3, 1536)
        lru_p[:, b] = r["o_lru_p"].transpose(0, 2, 1).reshape(2, D)
        lruc_p[:, b] = r["o_lruc_p"].transpose(0, 3, 2, 1).reshape(2, 3, D)
        sl = slice(b * NSAMP, (b + 1) * NSAMP)
        ssd_s[:, sl] = r["o_ssd_s"].reshape(2, NSAMP, 16, 64, 128)
        ssdc_s[:, sl] = r["o_ssdc_s"].transpose(0, 3, 4, 2, 1).reshape(2, NSAMP, 3, 1536)
        lru_s[:, sl] = r["o_lru_s"].transpose(0, 3, 2, 1).reshape(2, NSAMP, D)
        lruc_s[:, sl] = r["o_lruc_s"].transpose(0, 3, 4, 2, 1).reshape(2, NSAMP, 3, D)
    return (y_prompt, y_sample, ssd_p, ssdc_p, lru_p, lruc_p, ssd_s, ssdc_s, lru_s, lruc_s)
```
